# Optimizing a Trainium2 kernel written in Bass

```python
import jax, jax.numpy as jnp
from jax import lax
import numpy as np

D_MODEL = 1024
BATCH = 8
SEQ = 8192
DEPTH = 1
DEC_BATCH = 1
DEC_SEQ = 16384
PAST_LEN = 128

POOL_WIDTH = 256
POOL_WINDOWS = (2, 4, 8, 16)
N_POOL_GROUPS = 4
POOL_GROUP_DIM = POOL_WIDTH // N_POOL_GROUPS
N_HEADS = 6
QK_NOPE_DIM = 128
QK_ROPE_DIM = 64
QK_HEAD_DIM = QK_NOPE_DIM + QK_ROPE_DIM
V_HEAD_DIM = 128
ATTN_WIDTH = N_HEADS * V_HEAD_DIM
MIX_WIDTH = POOL_WIDTH + ATTN_WIDTH
Q_LORA_RANK = 384
KV_LORA_RANK = 256
IN_WIDTH = POOL_WIDTH + Q_LORA_RANK + KV_LORA_RANK + QK_ROPE_DIM
D_FF = -(-8 * D_MODEL // (3 * 256)) * 256
Q_BLOCK = 128
ROPE_THETA = 10000.0
EPS = 1e-6
N_MOD = 6

kernel_name = 'hybrid_pool_mla_sandwich_adaln_encoder'


def rms_norm(x, g):
    xf = x.astype(jnp.float32)
    y = xf * lax.rsqrt(jnp.mean(xf * xf, axis=-1, keepdims=True) + EPS)
    return (y * g.astype(jnp.float32)).astype(x.dtype)


def rope_tables(S):
    inv_freq = ROPE_THETA ** (-jnp.arange(0, QK_ROPE_DIM, 2, dtype=jnp.float32) / QK_ROPE_DIM)
    ang = jnp.arange(S, dtype=jnp.float32)[:, None] * inv_freq[None, :]
    return jnp.cos(ang), jnp.sin(ang)


def apply_rope(x, cos, sin):
    half = QK_ROPE_DIM // 2
    x1, x2 = x[..., :half], x[..., half:]
    cos = cos.astype(x.dtype)
    sin = sin.astype(x.dtype)
    return jnp.concatenate([x1 * cos - x2 * sin, x2 * cos + x1 * sin], axis=-1)


def multiscale_pool(u, pool_w, pool_scale):
    B, S, _ = u.shape
    ug = u.astype(jnp.float32).reshape(B, S, N_POOL_GROUPS, POOL_GROUP_DIM)
    cs = jnp.concatenate([jnp.zeros((B, 1, N_POOL_GROUPS, POOL_GROUP_DIM), jnp.float32),
                          jnp.cumsum(ug, axis=1)], axis=1)
    t = jnp.arange(S)[:, None]
    w = jnp.array(POOL_WINDOWS, dtype=jnp.int32)[None, :]
    lo = jnp.clip(t - w // 2, 0, S)
    hi = jnp.clip(t + w - w // 2, 0, S)
    gidx = jnp.arange(N_POOL_GROUPS)[None, :]
    win_sum = cs[:, hi, gidx] - cs[:, lo, gidx]
    count = (hi - lo).astype(jnp.float32)[None, :, :, None]
    pooled = win_sum / count - ug
    mixed = jnp.einsum('bsgc,gcd->bsgd', pooled.astype(u.dtype), pool_w)
    return mixed.reshape(B, S, POOL_WIDTH) * pool_scale


def mla_attention(cq, ckv, k_pe, g_q_a, w_uq, g_kv_a, w_ukv):
    B, S, _ = cq.shape
    cos, sin = rope_tables(S)
    q = (rms_norm(cq, g_q_a) @ w_uq).reshape(B, S, N_HEADS, QK_HEAD_DIM)
    kv = (rms_norm(ckv, g_kv_a) @ w_ukv).reshape(B, S, N_HEADS, QK_NOPE_DIM + V_HEAD_DIM)
    scale = QK_HEAD_DIM ** -0.5
    q_nope = q[..., :QK_NOPE_DIM] * scale
    q_pe = apply_rope(q[..., QK_NOPE_DIM:], cos[:, None, :], sin[:, None, :]) * scale
    k_nope = kv[..., :QK_NOPE_DIM]
    v = kv[..., QK_NOPE_DIM:]
    k_rot = apply_rope(k_pe, cos, sin)
    nb = S // Q_BLOCK
    qn_b = q_nope.reshape(B, nb, Q_BLOCK, N_HEADS, QK_NOPE_DIM).transpose(1, 0, 2, 3, 4)
    qr_b = q_pe.reshape(B, nb, Q_BLOCK, N_HEADS, QK_ROPE_DIM).transpose(1, 0, 2, 3, 4)

    def block(args):
        qn, qr = args
        s = (jnp.einsum('bqhd,bkhd->bhqk', qn, k_nope)
             + jnp.einsum('bqhr,bkr->bhqk', qr, k_rot)).astype(jnp.float32)
        p = jax.nn.softmax(s, axis=-1).astype(v.dtype)
        return jnp.einsum('bhqk,bkhd->bqhd', p, v)

    o = lax.map(block, (qn_b, qr_b))
    return o.transpose(1, 0, 2, 3, 4).reshape(B, S, ATTN_WIDTH)


def encoder_layer(x, c, w_ada, b_ada, g_mix_pre, g_mix_post, w_in, pool_w, pool_scale,
                  g_q_a, w_uq, g_kv_a, w_ukv, g_pool_out, g_attn_out, w_out,
                  g_ffn_pre, g_ffn_post, w_gate, w_up, w_down):
    B, S, D = x.shape
    mod = (jax.nn.silu(c) @ w_ada + b_ada).reshape(B, N_MOD, 1, D)
    shift1, scale1, gate1 = mod[:, 0], mod[:, 1], mod[:, 2]
    shift2, scale2, gate2 = mod[:, 3], mod[:, 4], mod[:, 5]

    h = rms_norm(x, g_mix_pre) * (1 + scale1) + shift1
    z = h @ w_in
    u, cq, ckv, k_pe = jnp.split(
        z, [POOL_WIDTH, POOL_WIDTH + Q_LORA_RANK, POOL_WIDTH + Q_LORA_RANK + KV_LORA_RANK], axis=-1)
    pool_out = multiscale_pool(u, pool_w, pool_scale)
    attn_out = mla_attention(cq, ckv, k_pe, g_q_a, w_uq, g_kv_a, w_ukv)
    merged = jnp.concatenate([rms_norm(pool_out, g_pool_out),
                              rms_norm(attn_out, g_attn_out)], axis=-1) @ w_out
    x = x + gate1 * rms_norm(merged, g_mix_post)

    h = rms_norm(x, g_ffn_pre) * (1 + scale2) + shift2
    f = (jax.nn.silu(h @ w_gate) * (h @ w_up)) @ w_down
    return x + gate2 * rms_norm(f, g_ffn_post)


def setup_inputs(seed: int = 0) -> dict:
    key = jax.random.key(seed)
    ks = jax.random.split(key, 32)
    f32 = jnp.float32
    L = DEPTH

    def nrm(k, shape, scale):
        return jax.random.normal(k, shape, f32) * scale

    def gain(k, n):
        return 1.0 + 0.05 * jax.random.normal(k, (L, n), f32)

    return {
        'x_prompt': nrm(ks[0], (BATCH, SEQ, D_MODEL), 1.0),
        'x_sample': nrm(ks[1], (DEC_BATCH, DEC_SEQ, D_MODEL), 1.0),
        'c_prompt': nrm(ks[2], (BATCH, D_MODEL), 1.0),
        'c_sample': nrm(ks[3], (DEC_BATCH, D_MODEL), 1.0),
        'w_ada': nrm(ks[4], (L, D_MODEL, N_MOD * D_MODEL), D_MODEL ** -0.5),
        'b_ada': nrm(ks[5], (L, N_MOD * D_MODEL), 0.1),
        'g_mix_pre': gain(ks[6], D_MODEL),
        'g_mix_post': gain(ks[7], D_MODEL),
        'w_in': nrm(ks[8], (L, D_MODEL, IN_WIDTH), D_MODEL ** -0.5),
        'pool_w': nrm(ks[9], (L, N_POOL_GROUPS, POOL_GROUP_DIM, POOL_GROUP_DIM), POOL_GROUP_DIM ** -0.5),
        'pool_scale': gain(ks[10], POOL_WIDTH),
        'g_q_a': gain(ks[11], Q_LORA_RANK),
        'w_uq': nrm(ks[12], (L, Q_LORA_RANK, N_HEADS * QK_HEAD_DIM), Q_LORA_RANK ** -0.5),
        'g_kv_a': gain(ks[13], KV_LORA_RANK),
        'w_ukv': nrm(ks[14], (L, KV_LORA_RANK, N_HEADS * (QK_NOPE_DIM + V_HEAD_DIM)), KV_LORA_RANK ** -0.5),
        'g_pool_out': gain(ks[15], POOL_WIDTH),
        'g_attn_out': gain(ks[16], ATTN_WIDTH),
        'w_out': nrm(ks[17], (L, MIX_WIDTH, D_MODEL), MIX_WIDTH ** -0.5),
        'g_ffn_pre': gain(ks[18], D_MODEL),
        'g_ffn_post': gain(ks[19], D_MODEL),
        'w_gate': nrm(ks[20], (L, D_MODEL, D_FF), D_MODEL ** -0.5),
        'w_up': nrm(ks[21], (L, D_MODEL, D_FF), D_MODEL ** -0.5),
        'w_down': nrm(ks[22], (L, D_FF, D_MODEL), D_FF ** -0.5),
    }


def reference(x_prompt, x_sample, c_prompt, c_sample, w_ada, b_ada, g_mix_pre, g_mix_post,
              w_in, pool_w, pool_scale, g_q_a, w_uq, g_kv_a, w_ukv, g_pool_out, g_attn_out,
              w_out, g_ffn_pre, g_ffn_post, w_gate, w_up, w_down):
    def run(x, c):
        for l in range(DEPTH):
            x = encoder_layer(x, c, w_ada[l], b_ada[l], g_mix_pre[l], g_mix_post[l], w_in[l],
                              pool_w[l], pool_scale[l], g_q_a[l], w_uq[l], g_kv_a[l], w_ukv[l],
                              g_pool_out[l], g_attn_out[l], w_out[l], g_ffn_pre[l], g_ffn_post[l],
                              w_gate[l], w_up[l], w_down[l])
        return x

    y_prompt = run(x_prompt, c_prompt)
    y_sample = run(x_sample, c_sample)
    return (y_prompt, y_sample)
```

```python
import os
import numpy as np
from contextlib import ExitStack
import concourse.bass as bass
import concourse.mybir as mybir
from concourse.bass_utils import run_bass_kernel_spmd

F32 = mybir.dt.float32
BF16 = mybir.dt.bfloat16
AF = mybir.ActivationFunctionType
ALU = mybir.AluOpType

D = 1024
DFF = 2816
NFF = 22
NH = 6
EPS = 1e-6
QSCALE = 192.0 ** -0.5
NCORES = 8
ENGS = ("pe", "act", "dve", "pool", "sp")
STRICT_SAME_ENGINE = os.environ.get("KSTRICT") == "1"


class Buf:
    __slots__ = ("name", "writers", "readers", "war")

    def __init__(self, name):
        self.name = name
        self.writers = []
        self.readers = []
        self.war = []


class T:
    __slots__ = ("ap", "b")

    def __init__(self, ap, name):
        self.ap = ap
        self.b = Buf(name)


class Op:
    __slots__ = ("eng", "fn", "deps", "needs_inc", "ticket", "dma", "semkey", "cpos", "barrier")

    def __init__(self, eng, fn, dma, semkey):
        self.eng = eng
        self.fn = fn
        self.deps = []
        self.needs_inc = False
        self.ticket = 0
        self.dma = dma
        self.semkey = semkey
        self.cpos = 0
        self.barrier = False


class Prog:
    def __init__(self):
        self.q = {e: [] for e in ENGS}
        self.ncomp = {e: 0 for e in ENGS}
        self.dma_count = {}
        self.last_dma = {}
        self.last_comp = {}
        self.out_ops = []
        self.nobarrier = set(("s_wd", "s_wg", "s_wu"))

    def op(self, eng, fn, reads=(), writes=(), dma=False, semkey=None, is_out=False):
        o = Op(eng, fn, dma, semkey)
        deps = set()
        for b in reads:
            for w in b.writers:
                deps.add(w)
        for b in writes:
            if b.readers:
                for r in b.readers:
                    deps.add(r)
                for w in b.writers:
                    deps.add(w)
            else:
                for r in b.war:
                    deps.add(r)
        deps.discard(o)
        o.deps = list(deps)
        for b in writes:
            if b.readers:
                b.war = b.readers
                b.readers = []
                b.writers = [o]
            else:
                b.writers.append(o)
        for b in reads:
            if b not in writes:
                b.readers.append(o)
        if not dma:
            self.ncomp[eng] += 1
            self.last_comp[eng] = o
        o.cpos = self.ncomp[eng]
        self.q[eng].append(o)
        if dma:
            c = self.dma_count.get(semkey, 0) + 16
            self.dma_count[semkey] = c
            o.ticket = c
            o.needs_inc = True
            self.last_dma[semkey] = o
        if is_out:
            self.out_ops.append(o)
        return o

    def barrier(self):
        front = list(self.last_comp.values()) + [o for k, o in self.last_dma.items() if k not in self.nobarrier]
        for e in ENGS:
            o = Op(e, None, False, None)
            o.barrier = True
            o.deps = [x for x in front if x.dma or x.eng != e]
            o.cpos = self.ncomp[e]
            self.q[e].append(o)

    @staticmethod
    def _need_sem(x, y):
        if x.dma:
            return True
        if x.eng == y.eng:
            if x.eng == "pe":
                return False
            if y.dma:
                return True
            if x.eng in ("act", "dve"):
                return STRICT_SAME_ENGINE or (y.cpos - x.cpos) < 3
            return True
        return True

    def finalize(self, nc, stack):
        for e in ENGS:
            for y in self.q[e]:
                y.deps = [x for x in y.deps if self._need_sem(x, y)]
                for x in y.deps:
                    if not x.dma:
                        x.needs_inc = True
        for e in ENGS:
            t = 0
            for o in self.q[e]:
                if not o.dma and o.needs_inc:
                    t += 1
                    o.ticket = t
        self.esem = {e: stack.enter_context(nc.semaphore("s_" + e)) for e in ENGS}
        self.dsem = {k: stack.enter_context(nc.semaphore("d_%d" % i)) for i, k in enumerate(self.dma_count)}

    def emit(self, eng_name, eng):
        waited = {}
        for o in self.q[eng_name]:
            need = {}
            for x in o.deps:
                s = self.dsem[x.semkey] if x.dma else self.esem[x.eng]
                k = id(s)
                if k not in need or need[k][1] < x.ticket:
                    need[k] = (s, x.ticket)
            for k, (s, v) in need.items():
                if waited.get(k, 0) >= v:
                    continue
                eng.wait_ge(s, v)
                waited[k] = v
            if o.fn is None:
                continue
            ins = o.fn(eng)
            if o.needs_inc:
                if o.dma:
                    ins.then_inc(self.dsem[o.semkey], 16)
                else:
                    ins.then_inc(self.esem[o.eng], 1)

    def final_waits(self, eng):
        done = {}
        for o in self.out_ops:
            done[o.semkey] = max(done.get(o.semkey, 0), o.ticket)
        for k, v in done.items():
            eng.wait_ge(self.dsem[k], v)


class Arena:
    def __init__(self, tensor, nbytes):
        self.t = tensor
        self.cap = nbytes
        self.off = 0
        self.peak = 0

    def alloc(self, name, shape, dtype):
        esz = 4 if dtype == F32 else 2
        n = 1
        for s in shape:
            n *= s
        nb = n * esz
        off = (self.off + 63) // 64 * 64
        assert off + nb <= self.cap, "arena overflow at %s: %d + %d > %d" % (name, off, nb, self.cap)
        self.off = off + nb
        self.peak = max(self.peak, self.off)
        v = self.t[:, off // 2:(off + nb) // 2]
        if dtype == F32:
            v = v.bitcast(F32)
        if len(shape) == 2:
            v = v.rearrange("p (a b) -> p a b", a=shape[0])
        elif len(shape) == 3:
            v = v.rearrange("p (a b c) -> p a b c", a=shape[0], b=shape[1])
        return T(v, name)

    def mark(self):
        return self.off

    def release(self, m):
        self.off = m


class Cfg:
    def __init__(self, SP=8192, SS=16384, NQB=4):
        self.SP, self.SS, self.NQB = SP, SS, NQB
        self.QJ = 512 * NQB
        self.NJP = SP // self.QJ
        assert SS // NCORES == self.QJ and SP % self.QJ == 0
        self.SMAX = max(SP, SS)


def build(cfg):
    SP, SS, NQB, QJ, NJP, SMAX = cfg.SP, cfg.SS, cfg.NQB, cfg.QJ, cfg.NJP, cfg.SMAX
    nc = bass.Bass("TRN2", target_bir_lowering=False)

    def din(name, shape):
        return nc.dram_tensor(name, list(shape), F32, kind="ExternalInput").ap()

    xp_d = din("xp", (SP + 16, D))
    xs_d = din("xs", (SS, D))
    xsq_d = din("xsq", (QJ + 16, D))
    cvec_d = din("cvec", (2, 128, 8))
    wada_d = din("w_ada", (D, 6 * D))
    badac_d = din("b_ada_col", (128, 48))
    badar_d = din("b_ada_row", (6 * D,))
    gcols_d = din("gcols", (128, 31))
    gpost_d = din("gpost", (2, D))
    win_d = din("w_in_x", (D, 1024))
    wuq_d = din("w_uq_x", (384, 1536))
    wukv_d = din("w_ukv", (256, 1536))
    wout_d = din("w_out", (D, D))
    wg_d = din("w_gate", (D, DFF))
    wu_d = din("w_up", (D, DFF))
    wd_d = din("w_down", (DFF, D))
    pwbd_d = din("pw_bd", (2, 128, 128))
    cosk_d = din("cosk", (64, SMAX))
    sink_d = din("sink", (64, SMAX))
    cosq_d = din("cosq", (64, QJ))
    sinq_d = din("sinq", (64, QJ))
    invc_d = din("invc", (1 + NJP, 128, 2, QJ))
    hmask_d = din("hmask", (1 + NJP, 128, 16))
    ident_d = din("ident", (128, 128))
    yp_d = nc.dram_tensor("yp", [SP, D], F32, kind="ExternalOutput").ap()
    ys_d = nc.dram_tensor("ys", [QJ, D], F32, kind="ExternalOutput").ap()

    def dscr(name, shape):
        return T(nc.dram_tensor(name, list(shape), BF16).ap(), name)

    s_win = dscr("s_win", (128, 8, 1024))
    s_wuq = dscr("s_wuq", (128, 3, 1536))
    s_wukv = dscr("s_wukv", (128, 2, 1536))
    s_wout = dscr("s_wout", (128, 8, 1024))
    s_wd = dscr("s_wd", (128, NFF, 1024))
    s_wgu = dscr("s_wgu", (NFF, 128, 2, 8, 128))
    lat_ckv = dscr("lat_ckv", (128, 2, SMAX))
    lat_kr = dscr("lat_kr", (64, SMAX))
    b_yout = Buf("yout")

    P = Prog()
    st = ExitStack()
    ARENA_BYTES = 207 * 1024
    arena_t = st.enter_context(nc.sbuf_tensor("arena", [128, ARENA_BYTES // 2], BF16))
    A = Arena(arena_t, ARENA_BYTES)
    banks = [T(st.enter_context(nc.psum_tensor("bk%d" % i, [128, 512], F32))[:], "bk%d" % i) for i in range(8)]

    def mm(out, lhsT, rhs, start, stop, R, W):
        P.op("pe", lambda e: e.matmul(out, lhsT=lhsT, rhs=rhs, start=start, stop=stop), reads=R, writes=W)

    def tr(out, in_, ident, R, W):
        P.op("pe", lambda e: e.transpose(out=out, in_=in_, identity=ident), reads=R, writes=W)

    def act(out, in_, func, R, W, **kw):
        P.op("act", lambda e: e.activation(out=out, in_=in_, func=func, **kw), reads=R, writes=W)

    def tt(out, in0, in1, op, R, W, eng="dve"):
        P.op(eng, lambda e: e.tensor_tensor(out=out, in0=in0, in1=in1, op=op), reads=R, writes=W)

    def ts(out, in0, s1, s2, op0, op1, R, W, eng="dve"):
        if op1 is None:
            P.op(eng, lambda e: e.tensor_scalar(out=out, in0=in0, scalar1=s1, scalar2=None, op0=op0),
                 reads=R, writes=W)
        else:
            P.op(eng, lambda e: e.tensor_scalar(out=out, in0=in0, scalar1=s1, scalar2=s2, op0=op0, op1=op1),
                 reads=R, writes=W)

    def stt(out, in0, scalar, in1, op0, op1, R, W, eng="dve"):
        P.op(eng, lambda e: e.scalar_tensor_tensor(out=out, in0=in0, scalar=scalar, in1=in1, op0=op0, op1=op1),
             reads=R, writes=W)

    def cp(out, in_, R, W, eng="dve"):
        P.op(eng, lambda e: e.tensor_copy(out=out, in_=in_), reads=R, writes=W)

    def memset(ap, val, W, eng="dve"):
        P.op(eng, lambda e: e.memset(ap, val), writes=W)

    def dma(eng, out, in_, R, W, key, is_out=False):
        P.op(eng, lambda e: e.dma_start(out=out, in_=in_), reads=R, writes=W, dma=True, semkey=key, is_out=is_out)

    idb = A.alloc("idb", (128,), BF16)
    ones_b = A.alloc("ones_b", (128,), BF16)
    ones_f = A.alloc("ones_f", (128,), F32)
    epsc = A.alloc("epsc", (1,), F32)
    gcols = A.alloc("gcols", (31,), F32)
    badac = A.alloc("badac", (48,), F32)
    modc = A.alloc("modc", (32,), F32)
    A1c = A.alloc("A1c", (8,), F32)
    A2c = A.alloc("A2c", (8,), F32)
    G1b = A.alloc("G1b", (D,), F32)
    G2b = A.alloc("G2b", (D,), F32)
    pwbd = A.alloc("pwbd", (2, 128), BF16)
    poolT = A.alloc("poolT", (2, QJ), BF16)
    attnT = A.alloc("attnT", (NH, QJ), BF16)
    G_Q, G_KV, G_PS, G_OUT = 16, 19, 21, 23
    CQ_MARK = A.mark()
    cqT = A.alloc("cqT", (3, QJ), BF16)
    PERSIST = A.mark()

    def init():
        idf = A.alloc("idf", (128,), F32)
        dma("sp", idf.ap, ident_d, [], [idf.b], "idf")
        cp(idb.ap, idf.ap, [idf.b], [idb.b])
        memset(ones_b.ap, 1.0, [ones_b.b])
        memset(ones_f.ap, 1.0, [ones_f.b])
        memset(epsc.ap, EPS, [epsc.b])
        dma("pool", pwbd.ap, pwbd_d.rearrange("c p m -> p c m"), [], [pwbd.b], "pwbd")
        dma("pool", s_win.ap, win_d.rearrange("(c p) m -> p c m", p=128), [], [s_win.b], "s_win")
        dma("pool", s_wuq.ap, wuq_d.rearrange("(c p) m -> p c m", p=128), [], [s_wuq.b], "s_wuq")
        dma("pool", s_wukv.ap, wukv_d.rearrange("(c p) m -> p c m", p=128), [], [s_wukv.b], "s_wukv")
        stf = [A.alloc("stf%d" % i, (D,), F32) for i in range(2)]
        stb = [A.alloc("stb%d" % i, (D,), BF16) for i in range(2)]
        for c in range(8):
            i = c % 2
            dma("sp", stf[i].ap, wout_d[c * 128:(c + 1) * 128, :], [], [stf[i].b], "stf%d" % i)
            ts(stb[i].ap, stf[i].ap, gcols.ap[:, G_OUT + c:G_OUT + c + 1], None, ALU.mult, None,
               [stf[i].b, gcols.b], [stb[i].b])
            dma("sp", s_wout.ap[:, c, :], stb[i].ap, [stb[i].b], [s_wout.b], "stb%d" % i)

    def init_b():
        dma("pool", s_wd.ap, wd_d.rearrange("(j p) n -> p j n", p=128), [], [s_wd.b], "s_wd")
        for j in range(NFF):
            dma("pool", s_wgu.ap[j, :, 0], wg_d[:, j * 128:(j + 1) * 128].rearrange("(c p) m -> p c m", p=128),
                [], [s_wgu.b], "s_wg")
            dma("pool", s_wgu.ap[j, :, 1], wu_d[:, j * 128:(j + 1) * 128].rearrange("(c p) m -> p c m", p=128),
                [], [s_wgu.b], "s_wu")

    s_modc = T(nc.dram_tensor("s_modc", [128, 48], F32).ap(), "s_modc")
    s_G = T(nc.dram_tensor("s_G", [128, 2, D], F32).ap(), "s_G")

    def mod_both():
        wad = [A.alloc("wad%d" % i, (8, 512), BF16) for i in range(12)]
        order = [(mi, half) for mi in (0, 1, 3, 4) for half in range(2)] + [(mi, half) for mi in (2, 5) for half in range(2)]
        for nb, (mi, half) in enumerate(order):
            col0 = mi * D + half * 512
            dma("pool", wad[nb].ap, wada_d[:, col0:col0 + 512].rearrange("(c p) m -> p c m", p=128),
                [], [wad[nb].b], "wad%d" % nb)
        rowb = [A.alloc("rowb%d" % i, (512,), F32) for i in range(4)]
        rowg = [A.alloc("rowg%d" % i, (512,), F32) for i in range(4)]
        for k, (mi, half) in enumerate(order[8:]):
            col0 = mi * D + half * 512
            dma("sp", rowb[k].ap, badar_d[col0:col0 + 512].partition_broadcast(128), [], [rowb[k].b], "rowb%d" % k)
            dma("sp", rowg[k].ap, gpost_d[k // 2, half * 512:(half + 1) * 512].partition_broadcast(128),
                [], [rowg[k].b], "rowg%d" % k)
        tmpc = A.alloc("m_tmpc", (48,), F32)
        tmpG = A.alloc("m_tmpG", (2, D), F32)
        rsum = A.alloc("m_rsum", (512,), F32)
        for si in range(2):
            cc = A.alloc("cc%d" % si, (8,), F32)
            csb = A.alloc("csb%d" % si, (8,), BF16)
            lhsb = A.alloc("lhsb%d" % si, (8, 128), BF16)
            dma("sp", cc.ap, cvec_d[si], [], [cc.b], "cc%d" % si)
            act(csb.ap, cc.ap, AF.Silu, [cc.b], [csb.b])
            for c in range(8):
                cp(lhsb.ap[:, c, :], csb.ap[:, c:c + 1].to_broadcast([128, 128]), [csb.b], [lhsb.b])
            pcol = banks[si]
            colmods = (0, 1, 3, 4)
            for nb in range(8):
                ci, half = nb // 2, nb % 2
                w = wad[nb]
                for m4 in range(4):
                    idx = ci * 8 + half * 4 + m4
                    for c in range(8):
                        mm(pcol.ap[:, idx:idx + 1], w.ap[:, c, m4 * 128:(m4 + 1) * 128], csb.ap[:, c:c + 1],
                           c == 0, c == 7, [w.b, csb.b], [pcol.b])
            if si == 0:
                mc, a1, a2, mcb, a1b, a2b = modc.ap, A1c.ap, A2c.ap, modc.b, A1c.b, A2c.b
            else:
                mc, a1, a2 = tmpc.ap[:, 0:32], tmpc.ap[:, 32:40], tmpc.ap[:, 40:48]
                mcb = a1b = a2b = tmpc.b
            for ci, mi in enumerate(colmods):
                tt(mc[:, ci * 8:(ci + 1) * 8], pcol.ap[:, ci * 8:(ci + 1) * 8], badac.ap[:, mi * 8:(mi + 1) * 8],
                   ALU.add, [pcol.b, badac.b], [mcb])
            stt(a1, mc[:, 8:16], 1.0, gcols.ap[:, 0:8], ALU.add, ALU.mult, [mcb, gcols.b], [a1b])
            stt(a2, mc[:, 24:32], 1.0, gcols.ap[:, 8:16], ALU.add, ALU.mult, [mcb, gcols.b], [a2b])
            for k in range(4):
                gi, half = k // 2, k % 2
                w = wad[8 + k]
                pr = banks[2 + (k % 2) + 2 * si]
                for c in range(8):
                    mm(pr.ap, lhsb.ap[:, c, :], w.ap[:, c, :], c == 0, c == 7, [lhsb.b, w.b], [pr.b])
                tt(rsum.ap, pr.ap, rowb[k].ap, ALU.add, [pr.b, rowb[k].b], [rsum.b])
                if si == 0:
                    G = (G1b, G2b)[gi]
                    tt(G.ap[:, half * 512:(half + 1) * 512], rsum.ap, rowg[k].ap, ALU.mult, [rsum.b, rowg[k].b], [G.b])
                else:
                    tt(tmpG.ap[:, gi, half * 512:(half + 1) * 512], rsum.ap, rowg[k].ap, ALU.mult,
                       [rsum.b, rowg[k].b], [tmpG.b])
            if si == 1:
                dma("sp", s_modc.ap, tmpc.ap, [tmpc.b], [s_modc.b], "m_oc")
                dma("sp", s_G.ap, tmpG.ap, [tmpG.b], [s_G.b], "m_oG")

    def load_mod1():
        dma("sp", modc.ap, s_modc.ap[:, 0:32], [s_modc.b], [modc.b], "l_modc")
        dma("sp", A1c.ap, s_modc.ap[:, 32:40], [s_modc.b], [A1c.b], "l_a1")
        dma("sp", A2c.ap, s_modc.ap[:, 40:48], [s_modc.b], [A2c.b], "l_a2")
        dma("sp", G1b.ap, s_G.ap[:, 0, :], [s_G.b], [G1b.b], "l_g1")
        dma("sp", G2b.ap, s_G.ap[:, 1, :], [s_G.b], [G2b.b], "l_g2")

    B1c = modc.ap[:, 0:8]
    B2c = modc.ap[:, 16:24]

    class Front:
        def __init__(self, tag, nx=8, nxn=8):
            self.xt = [A.alloc("%s_xt%d" % (tag, i), (D,), F32) for i in range(nx)]
            self.junk = [A.alloc("%s_junk%d" % (tag, i), (D,), BF16) for i in range(4)]
            self.xn = [A.alloc("%s_xn%d" % (tag, i), (D,), BF16) for i in range(nxn)]
            self.ss = [A.alloc("%s_ss%d" % (tag, i), (4,), F32) for i in range(2)]
            self.i = 0
            self.g = 0
            self.tag = tag

        def part1(self, tiles, n):
            k = len(tiles)
            ss = self.ss[self.g % 2]
            xns = [self.xn[(self.g % 2) * 4 + t] for t in range(k)] if len(self.xn) >= 8 else [self.xn[t] for t in range(k)]
            self.g += 1
            xts = []
            for srcs in tiles:
                xi = self.i % len(self.xt)
                self.i += 1
                xt = self.xt[xi]
                xts.append(xt)
                for (ap, r0, nr) in srcs:
                    dma("sp", xt.ap[r0:r0 + nr, :], ap, [], [xt.b], "%s_xt%d" % (self.tag, xi))
            for t, xt in enumerate(xts):
                act(self.junk[t].ap[0:n, :], xt.ap[0:n, :], AF.Square, [xt.b], [self.junk[t].b, ss.b],
                    scale=float(D) ** -0.5, accum_out=ss.ap[0:n, t:t + 1])
            act(ss.ap[0:n, 0:k], ss.ap[0:n, 0:k], AF.Ln, [ss.b, epsc.b], [ss.b], bias=epsc.ap[0:n, :])
            act(ss.ap[0:n, 0:k], ss.ap[0:n, 0:k], AF.Exp, [ss.b], [ss.b], scale=-0.5)
            for t, xt in enumerate(xts):
                act(xns[t].ap[0:n, :], xt.ap[0:n, :], AF.Copy, [xt.b, ss.b], [xns[t].b], scale=ss.ap[0:n, t:t + 1])
            return xns

        def part2_tile(self, xns, t, n, Acol, Bcol, ABb, hT, tps):
            xn = xns[t]
            tp = tps[t % len(tps)]
            tpv = tp.ap.bitcast(BF16).rearrange("p (c n) -> p c n", c=8)
            for c in range(8):
                tr(tpv[:, c, 0:n], xn.ap[0:n, c * 128:(c + 1) * 128], idb.ap[0:n, 0:n], [xn.b, idb.b], [tp.b])
            for c in range(8):
                ts(hT.ap[:, c, t * n:(t + 1) * n], tpv[:, c, 0:n], Acol[:, c:c + 1], Bcol[:, c:c + 1],
                   ALU.mult, ALU.add, [tp.b] + ABb, [hT.b])

        def part2(self, xns, n, Acol, Bcol, ABb, hT, tps):
            for t in range(len(xns)):
                self.part2_tile(xns, t, n, Acol, Bcol, ABb, hT, tps)

        def run_block(self, tiles, n, Acol, Bcol, ABb, hT, tps):
            xns = self.part1(tiles, n)
            self.part2(xns, n, Acol, Bcol, ABb, hT, tps)

    def rms_sq(pss, nchunk, sq):
        for c in range(nchunk):
            act(sq.ap[:, c, :], pss[c].ap, AF.Square, [pss[c].b], [sq.b])

    def rms_fin(nchunk, inv_n, sq, ssb, rstdb):
        for c in range(nchunk):
            mm(ssb.ap, ones_b.ap, sq.ap[:, c, :], c == 0, c == nchunk - 1, [ones_b.b, sq.b], [ssb.b])
        act(rstdb.ap, ssb.ap, AF.Ln, [ssb.b, epsc.b], [rstdb.b], scale=inv_n, bias=epsc.ap)
        act(rstdb.ap, rstdb.ap, AF.Exp, [rstdb.b], [rstdb.b], scale=-0.5)

    def rms_bcast(pss, nchunk, inv_n, sq, ssb, rstdb):
        rms_sq(pss, nchunk, sq)
        rms_fin(nchunk, inv_n, sq, ssb, rstdb)

    def phase_a(x_rows, S):
        m = A.mark()
        win = A.alloc("a_win", (8, 384), BF16)
        dma("sp", win.ap, s_win.ap[:, :, 640:1024], [s_win.b], [win.b], "a_win")
        fr = Front("a")
        hT = [A.alloc("a_hT%d" % i, (8, 512), BF16) for i in range(2)]
        sq = A.alloc("a_sq", (2, 512), BF16)
        rstdb = A.alloc("a_rstdb", (512,), F32)
        ckvn = [A.alloc("a_ckvn%d" % i, (2, 512), BF16) for i in range(2)]
        krot = [A.alloc("a_krot%d" % i, (512,), BF16) for i in range(2)]
        cst = [A.alloc("a_cos%d" % i, (512,), F32) for i in range(2)]
        snt = [A.alloc("a_sin%d" % i, (512,), F32) for i in range(2)]
        t1 = A.alloc("a_t1", (512,), F32)
        t2 = A.alloc("a_t2", (512,), F32)
        tps = [banks[0], banks[7]]
        b1, b2, b3, b4, b5 = banks[1], banks[2], banks[5], banks[6], banks[3]
        nblk = S // 512
        AB = [A1c.b, modc.b]

        def tiles_of(blk):
            return [[(x_rows(blk * 512 + t4 * 128, 128), 0, 128)] for t4 in range(4)]

        xq = {0: fr.part1(tiles_of(0), 128)}
        fr.part2(xq[0], 128, A1c.ap, B1c, AB, hT[0], tps)
        if nblk > 1:
            xq[1] = fr.part1(tiles_of(1), 128)
        for blk in range(nblk):
            h = hT[blk % 2]
            r = blk % 2
            dma("sp", cst[r].ap[0:64, :], cosk_d[:, blk * 512:(blk + 1) * 512], [], [cst[r].b], "a_cos%d" % r)
            dma("sp", snt[r].ap[0:64, :], sink_d[:, blk * 512:(blk + 1) * 512], [], [snt[r].b], "a_sin%d" % r)
            if blk + 2 < nblk:
                xq[blk + 2] = fr.part1(tiles_of(blk + 2), 128)
            for g, (c0, M, bk) in enumerate(((0, 128, b1), (128, 128, b2), (256, 64, b3), (320, 64, b4))):
                for c in range(8):
                    mm(bk.ap[0:M, :], win.ap[:, c, c0:c0 + M], h.ap[:, c, :], c == 0, c == 7, [win.b, h.b], [bk.b])
                if blk + 1 < nblk:
                    fr.part2_tile(xq[blk + 1], g, 128, A1c.ap, B1c, AB, hT[(blk + 1) % 2], tps)
            xq.pop(blk, None)
            rms_sq([b1, b2], 2, sq)
            rms_fin(2, 1.0 / 256, sq, b5, rstdb)
            for c, bk in enumerate((b1, b2)):
                stt(ckvn[r].ap[:, c, :], bk.ap, gcols.ap[:, G_KV + c:G_KV + c + 1], rstdb.ap, ALU.mult, ALU.mult,
                    [bk.b, gcols.b, rstdb.b], [ckvn[r].b])
            tt(t1.ap[0:64, :], b3.ap[0:64, :], cst[r].ap[0:64, :], ALU.mult, [b3.b, cst[r].b], [t1.b])
            tt(t2.ap[0:64, :], b4.ap[0:64, :], snt[r].ap[0:64, :], ALU.mult, [b4.b, snt[r].b], [t2.b])
            tt(krot[r].ap[0:64, :], t1.ap[0:64, :], t2.ap[0:64, :], ALU.add, [t1.b, t2.b], [krot[r].b])
            dma("pool", lat_ckv.ap[:, :, blk * 512:(blk + 1) * 512], ckvn[r].ap, [ckvn[r].b], [lat_ckv.b], "a_lo%d" % r)
            dma("pool", lat_kr.ap[:, blk * 512:(blk + 1) * 512], krot[r].ap[0:64, :], [krot[r].b], [lat_kr.b], "a_ko%d" % r)
        P.barrier()
        A.release(m)

    def phase_b(x_rows, halo_rows, jk):
        m = A.mark()
        win = A.alloc("b_win", (8, 640), BF16)
        dma("sp", win.ap, s_win.ap[:, :, 0:640], [s_win.b], [win.b], "b_win")
        fr = Front("b", nx=4, nxn=4)
        hT = [A.alloc("b_hT%d" % i, (8, 512), BF16) for i in range(2)]
        hTh = A.alloc("b_hTh", (8, 16), BF16)
        sq = A.alloc("b_sq", (3, 512), BF16)
        rstdb = A.alloc("b_rstdb", (512,), F32)
        N = QJ + 16
        uext = A.alloc("b_uext", (2, N), F32)
        sA = A.alloc("b_sA", (2, N), F32)
        sB = A.alloc("b_sB", (2, N), F32)
        invc = A.alloc("b_invc", (2, QJ), F32)
        hmask = A.alloc("b_hmask", (16,), F32)
        tmpw = A.alloc("b_tmpw", (QJ,), F32)
        pooledT = A.alloc("b_pooled", (2, QJ), BF16)
        dma("sp", invc.ap, invc_d[jk], [], [invc.b], "b_invc")
        dma("sp", hmask.ap, hmask_d[jk], [], [hmask.b], "b_hmask")
        tp = banks[0]
        ub = (banks[1], banks[2])
        qb_ = (banks[3], banks[4], banks[5])
        ssb = banks[6]
        b7 = banks[7]
        hl, hr = halo_rows
        fr.run_block([[(hl, 0, 8), (hr, 8, 8)]], 16, A1c.ap, B1c, [A1c.b, modc.b], hTh, [tp])

        AB = [A1c.b, modc.b]

        def tiles_of(blk):
            return [[(x_rows(blk * 512 + t4 * 128, 128), 0, 128)] for t4 in range(4)]

        xq = {0: fr.part1(tiles_of(0), 128)}
        fr.part2(xq[0], 128, A1c.ap, B1c, AB, hT[0], [tp])
        for cu in range(2):
            for c in range(8):
                mm(b7.ap[:, cu * 16:(cu + 1) * 16], win.ap[:, c, cu * 128:(cu + 1) * 128], hTh.ap[:, c, :],
                   c == 0, c == 7, [win.b, hTh.b], [b7.b])
        for cu in range(2):
            tt(uext.ap[:, cu, 0:8], b7.ap[:, cu * 16:cu * 16 + 8], hmask.ap[:, 0:8], ALU.mult,
               [b7.b, hmask.b], [uext.b])
            tt(uext.ap[:, cu, 8 + QJ:16 + QJ], b7.ap[:, cu * 16 + 8:cu * 16 + 16], hmask.ap[:, 8:16], ALU.mult,
               [b7.b, hmask.b], [uext.b])
        for blk in range(NQB):
            h = hT[blk % 2]
            if blk + 1 < NQB:
                xq[blk + 1] = fr.part1(tiles_of(blk + 1), 128)
            groups = [(ub[0], 0), (ub[1], 128), (qb_[0], 256), (qb_[1], 384), (qb_[2], 512)]
            for g, (bk, c0) in enumerate(groups):
                for c in range(8):
                    mm(bk.ap, win.ap[:, c, c0:c0 + 128], h.ap[:, c, :], c == 0, c == 7, [win.b, h.b], [bk.b])
                if g < 2:
                    act(uext.ap[:, g, 8 + blk * 512:8 + (blk + 1) * 512], bk.ap, AF.Copy, [bk.b], [uext.b])
                if blk + 1 < NQB and g < 4:
                    fr.part2_tile(xq[blk + 1], g, 128, A1c.ap, B1c, AB, hT[(blk + 1) % 2], [tp])
            xq.pop(blk, None)
            rms_bcast(list(qb_), 3, 1.0 / 384, sq, ssb, rstdb)
            for cq in range(3):
                stt(cqT.ap[:, cq, blk * 512:(blk + 1) * 512], qb_[cq].ap, gcols.ap[:, G_Q + cq:G_Q + cq + 1],
                    rstdb.ap, ALU.mult, ALU.mult, [qb_[cq].b, gcols.b, rstdb.b], [cqT.b])
        u = uext
        tt(sA.ap[:, :, 1:N], u.ap[:, :, 1:N], u.ap[:, :, 0:N - 1], ALU.add, [u.b], [sA.b])
        wins = []

        def pooled(src, rows, c, sh):
            tt(tmpw.ap[rows, :], src.ap[rows, c, sh:sh + QJ], invc.ap[rows, c, :], ALU.mult, [src.b, invc.b], [tmpw.b])
            tt(pooledT.ap[rows, c, :], tmpw.ap[rows, :], u.ap[rows, c, 8:8 + QJ], ALU.subtract,
               [tmpw.b, u.b], [pooledT.b])

        lo, hi = slice(0, 64), slice(64, 128)
        pooled(sA, lo, 0, 8)
        tt(sB.ap[:, :, 3:N], sA.ap[:, :, 3:N], sA.ap[:, :, 1:N - 2], ALU.add, [sA.b], [sB.b])
        pooled(sB, hi, 0, 9)
        tt(sA.ap[:, :, 7:N], sB.ap[:, :, 7:N], sB.ap[:, :, 3:N - 4], ALU.add, [sB.b], [sA.b])
        pooled(sA, lo, 1, 11)
        tt(sB.ap[:, :, 15:N], sA.ap[:, :, 15:N], sA.ap[:, :, 7:N - 8], ALU.add, [sA.b], [sB.b])
        pooled(sB, hi, 1, 15)
        for blk in range(NQB):
            for c in range(2):
                mm(b7.ap, pwbd.ap[:, c, :], pooledT.ap[:, c, blk * 512:(blk + 1) * 512], True, True,
                   [pwbd.b, pooledT.b], [b7.b])
                ts(poolT.ap[:, c, blk * 512:(blk + 1) * 512], b7.ap, gcols.ap[:, G_PS + c:G_PS + c + 1], None,
                   ALU.mult, None, [b7.b, gcols.b], [poolT.b])
        P.barrier()
        A.release(m)

    def phase_c(S, cos_src, sin_src):
        m = A.mark()
        wuq = A.alloc("c_wuq", (3, 1536), BF16)
        wukv = A.alloc("c_wukv", (2, 1536), BF16)
        dma("sp", wuq.ap, s_wuq.ap, [s_wuq.b], [wuq.b], "c_wuq")
        dma("sp", wukv.ap, s_wukv.ap, [s_wukv.b], [wukv.b], "c_wukv")
        cosq = A.alloc("c_cosq", (QJ,), F32)
        sinq = A.alloc("c_sinq", (QJ,), F32)
        dma("sp", cosq.ap[0:64, :], cos_src, [], [cosq.b], "c_cosq")
        dma("sp", sinq.ap[0:64, :], sin_src, [], [sinq.b], "c_sinq")
        ts(cosq.ap[0:64, :], cosq.ap[0:64, :], QSCALE, None, ALU.mult, None, [cosq.b], [cosq.b])
        ts(sinq.ap[0:64, :], sinq.ap[0:64, :], QSCALE, None, ALU.mult, None, [sinq.b], [sinq.b])
        Qn = [A.alloc("c_Qn%d" % i, (QJ,), BF16) for i in range(2)]
        Qr = [A.alloc("c_Qr%d" % i, (QJ,), BF16) for i in range(2)]
        for i in range(2):
            memset(Qr[i].ap[64:128, :], 0.0, [Qr[i].b])
        NR = 3
        ckb = [A.alloc("c_ckb%d" % i, (2, 512), BF16) for i in range(NR)]
        krb = [A.alloc("c_krb%d" % i, (512,), BF16) for i in range(NR)]
        for i in range(NR):
            memset(krb[i].ap[64:128, :], 0.0, [krb[i].b])
        knb = [A.alloc("c_knb%d" % i, (512,), BF16) for i in range(2)]
        vb = [A.alloc("c_vb%d" % i, (4, 128), BF16) for i in range(2)]
        NP = 2 * NQB + 4
        Pt = [A.alloc("c_P%d" % i, (512,), BF16) for i in range(NP)]
        tpair = [A.alloc("c_tp%d" % i, (512,), BF16) for i in range(2)]
        acc = [A.alloc("c_acc%d" % i, (512,), F32) for i in range(NQB)]
        rinv = A.alloc("c_rinv", (512,), F32)
        t1 = A.alloc("c_t1", (512,), F32)
        t2 = A.alloc("c_t2", (512,), F32)
        Ob = banks[0:NQB]
        Sb = banks[4:7]
        b7 = banks[7]
        NKB = S // 512
        L = 2

        def load_lat(kb, r):
            dma("sp", ckb[r].ap, lat_ckv.ap[:, :, kb * 512:(kb + 1) * 512], [lat_ckv.b], [ckb[r].b], "c_ckb%d" % r)
            dma("sp", krb[r].ap[0:64, :], lat_kr.ap[:, kb * 512:(kb + 1) * 512], [lat_kr.b], [krb[r].b], "c_krb%d" % r)

        def gen_k(h, kb):
            r, k2 = kb % NR, kb % 2
            for c in range(2):
                mm(b7.ap, wukv.ap[:, c, h * 256:h * 256 + 128], ckb[r].ap[:, c, :], c == 0, c == 1,
                   [wukv.b, ckb[r].b], [b7.b])
            act(knb[k2].ap, b7.ap, AF.Copy, [b7.b], [knb[k2].b])

        def gen_v(h, kb):
            r, k2 = kb % NR, kb % 2
            b7v = b7.ap.rearrange("p (t d) -> p t d", t=4)
            for t in range(4):
                for c in range(2):
                    mm(b7v[:, t, :], ckb[r].ap[:, c, t * 128:(t + 1) * 128], wukv.ap[:, c, h * 256 + 128:h * 256 + 256],
                       c == 0, c == 1, [wukv.b, ckb[r].b], [b7.b])
            cp(vb[k2].ap, b7v, [b7.b], [vb[k2].b])

        def gen_q(h):
            qi = h % 2
            bl = [Sb[0], Sb[1], Sb[2], b7]
            k = 0
            for qb in range(NQB):
                cs = slice(qb * 512, (qb + 1) * 512)
                bk = bl[k % 4]; k += 1
                for c in range(3):
                    mm(bk.ap, wuq.ap[:, c, h * 192:h * 192 + 128], cqT.ap[:, c, cs], c == 0, c == 2,
                       [wuq.b, cqT.b], [bk.b])
                act(Qn[qi].ap[:, cs], bk.ap, AF.Copy, [bk.b], [Qn[qi].b], scale=QSCALE)
                bk1 = bl[k % 4]; k += 1
                for c in range(3):
                    mm(bk1.ap[0:64, :], wuq.ap[:, c, h * 192 + 128:h * 192 + 192], cqT.ap[:, c, cs], c == 0, c == 2,
                       [wuq.b, cqT.b], [bk1.b])
                bk2 = bl[k % 4]; k += 1
                for c in range(3):
                    mm(bk2.ap[0:64, :], wuq.ap[:, c, 1152 + h * 64:1152 + (h + 1) * 64], cqT.ap[:, c, cs], c == 0, c == 2,
                       [wuq.b, cqT.b], [bk2.b])
                tt(t1.ap[0:64, :], bk1.ap[0:64, :], cosq.ap[0:64, cs], ALU.mult, [bk1.b, cosq.b], [t1.b])
                tt(t2.ap[0:64, :], bk2.ap[0:64, :], sinq.ap[0:64, cs], ALU.mult, [bk2.b, sinq.b], [t2.b])
                tt(Qr[qi].ap[0:64, cs], t1.ap[0:64, :], t2.ap[0:64, :], ALU.add, [t1.b, t2.b], [Qr[qi].b])

        for h in range(NH):
            qi = h % 2
            load_lat(0, 0)
            if NKB > 1:
                load_lat(1, 1)
            gen_q(h)
            gen_k(h, 0)
            gen_v(h, 0)
            items = [(kb, t, qb) for kb in range(NKB) for t in range(4) for qb in range(NQB)]
            n = len(items)
            per_kb = 4 * NQB
            for i in range(n + L):
                if i < n:
                    kb, t, qb = items[i]
                    if i % per_kb == 0:
                        if kb + 2 < NKB:
                            load_lat(kb + 2, (kb + 2) % NR)
                        if kb + 1 < NKB:
                            gen_k(h, kb + 1)
                    if i % per_kb == per_kb // 2 and kb + 1 < NKB:
                        gen_v(h, kb + 1)
                    r, k2 = kb % NR, kb % 2
                    cs = slice(qb * 512, (qb + 1) * 512)
                    sb_ = Sb[i % 3]
                    mm(sb_.ap, knb[k2].ap[:, t * 128:(t + 1) * 128], Qn[qi].ap[:, cs], True, False,
                       [knb[k2].b, Qn[qi].b], [sb_.b])
                    mm(sb_.ap, krb[r].ap[:, t * 128:(t + 1) * 128], Qr[qi].ap[:, cs], False, True,
                       [krb[r].b, Qr[qi].b], [sb_.b])
                    pt = Pt[i % NP]
                    act(pt.ap, sb_.ap, AF.Exp, [sb_.b], [pt.b])
                if i >= L:
                    i2 = i - L
                    kb, t, qb = items[i2]
                    k2 = kb % 2
                    kt = kb * 4 + t
                    pt = Pt[i2 % NP]
                    mm(Ob[qb].ap, vb[k2].ap[:, t, :], pt.ap, kt == 0, kt == NKB * 4 - 1, [vb[k2].b, pt.b], [Ob[qb].b])
                    if kt % 2 == 1:
                        pprev = Pt[(i2 - NQB) % NP]
                        if kt == 1:
                            tt(acc[qb].ap, pprev.ap, pt.ap, ALU.add, [pprev.b, pt.b], [acc[qb].b])
                        else:
                            tpb = tpair[(kt // 2) % 2]
                            tt(tpb.ap, pprev.ap, pt.ap, ALU.add, [pprev.b, pt.b], [tpb.b])
                            tt(acc[qb].ap, acc[qb].ap, tpb.ap, ALU.add, [acc[qb].b, tpb.b], [acc[qb].b])
            for qb in range(NQB):
                cs = slice(qb * 512, (qb + 1) * 512)
                mm(b7.ap, ones_f.ap, acc[qb].ap, True, True, [ones_f.b, acc[qb].b], [b7.b])
                act(rinv.ap, b7.ap, AF.Ln, [b7.b], [rinv.b])
                act(rinv.ap, rinv.ap, AF.Exp, [rinv.b], [rinv.b], scale=-1.0)
                tt(attnT.ap[:, h, cs], Ob[qb].ap, rinv.ap, ALU.mult, [Ob[qb].b, rinv.b], [attnT.b])
        P.barrier()
        A.release(m)

    def phase_d(x_rows, y_rows):
        m = A.mark()
        A.release(CQ_MARK)
        TB = 512
        NB = QJ // TB
        wout = A.alloc("d_wout", (8, D), BF16)
        wd = A.alloc("d_wd", (NFF, D), BF16)
        xt = [A.alloc("d_xt%d" % t, (D,), F32) for t in range(4)]
        mg = [A.alloc("d_mg%d" % i, (D,), F32) for i in range(2)]
        junks = [A.alloc("d_junk%d" % i, (D,), BF16) for i in range(2)]
        junk = junks[0]
        xn = [A.alloc("d_xn%d" % i, (D,), BF16) for i in range(2)]
        sqa = [A.alloc("d_sqa%d" % i, (8, 128), BF16) for i in range(2)]
        scs = [A.alloc("d_sc%d" % i, (4, 4), F32) for i in range(2)]
        sf = A.alloc("d_sf", (4,), F32)
        h2T = [A.alloc("d_h2T%d" % i, (8, TB), BF16) for i in range(2)]
        actT = A.alloc("d_actT", (NFF, TB), BF16)
        sg = [A.alloc("d_sg%d" % i, (TB,), F32) for i in range(2)]
        fst = [A.alloc("d_fst%d" % i, (D,), F32) for i in range(2)]
        x1b = mg
        NRW = 3
        wgu = [A.alloc("d_wgu%d" % i, (2, 8, 128), BF16) for i in range(NRW)]
        dma("sp", wout.ap, s_wout.ap, [s_wout.b], [wout.b], "d_wout")
        tps = [banks[0], banks[7]]
        b7 = banks[7]
        MPb, MAb = banks[1], banks[2]
        b_spill = [Buf("x1spill0"), Buf("x1spill1")]
        state = {"yi": 0, "wd_loaded": False}

        def d1_stages(b):
            st_ = b % 2
            sc = scs[st_]
            h2 = h2T[st_]
            stages = []

            def s1():
                for t in range(4):
                    q0 = b * TB + t * 128
                    qs = slice(q0, q0 + 128)
                    dma("sp", xt[t].ap, x_rows(q0, 128), [], [xt[t].b], "d_xt%d" % t)
                    sq_ = sqa[t % 2]
                    tt(sq_.ap[:, 0:2, :], poolT.ap[:, :, qs], poolT.ap[:, :, qs], ALU.mult, [poolT.b], [sq_.b])
                    tt(sq_.ap[:, 2:8, :], attnT.ap[:, :, qs], attnT.ap[:, :, qs], ALU.mult, [attnT.b], [sq_.b])
                    for c in range(2):
                        mm(b7.ap[:, t:t + 1], sq_.ap[:, c, :], ones_b.ap[:, 0:1], c == 0, c == 1, [sq_.b, ones_b.b], [b7.b])
                    for c in range(6):
                        mm(b7.ap[:, 4 + t:5 + t], sq_.ap[:, 2 + c, :], ones_b.ap[:, 0:1], c == 0, c == 5,
                           [sq_.b, ones_b.b], [b7.b])
                if not state["wd_loaded"]:
                    dma("sp", wd.ap, s_wd.ap, [s_wd.b], [wd.b], "d_wd")
                    state["wd_loaded"] = True
                act(sc.ap[:, 0, :], b7.ap[:, 0:4], AF.Ln, [b7.b, epsc.b], [sc.b], scale=1.0 / 256, bias=epsc.ap)
                act(sc.ap[:, 1, :], b7.ap[:, 4:8], AF.Ln, [b7.b, epsc.b], [sc.b], scale=1.0 / 768, bias=epsc.ap)
                act(sc.ap[:, 0:2, :], sc.ap[:, 0:2, :], AF.Exp, [sc.b], [sc.b], scale=-0.5)
            stages.append(s1)

            def mk_combo(t, half):
                def f():
                    q0 = b * TB + t * 128
                    qs = slice(q0, q0 + 128)
                    hs = slice(half * 512, (half + 1) * 512)
                    g_ = mg[t % 2]
                    for c in range(2):
                        mm(MPb.ap, poolT.ap[:, c, qs], wout.ap[:, c, hs], c == 0, c == 1, [poolT.b, wout.b], [MPb.b])
                    for c in range(6):
                        mm(MAb.ap, attnT.ap[:, c, qs], wout.ap[:, 2 + c, hs], c == 0, c == 5, [attnT.b, wout.b], [MAb.b])
                    ts(g_.ap[:, hs], MPb.ap, sc.ap[:, 0, t:t + 1], None, ALU.mult, None, [MPb.b, sc.b], [g_.b])
                    stt(g_.ap[:, hs], MAb.ap, sc.ap[:, 1, t:t + 1], g_.ap[:, hs], ALU.mult, ALU.add,
                        [MAb.b, sc.b, g_.b], [g_.b])
                return f

            def mk_s3(t0):
                def f():
                    for t in (t0, t0 + 1):
                        act(junks[t % 2].ap, mg[t % 2].ap, AF.Square, [mg[t % 2].b], [junks[t % 2].b, sc.b],
                            scale=float(D) ** -0.5, accum_out=sc.ap[:, 2, t:t + 1])
                    act(sc.ap[:, 2, t0:t0 + 2], sc.ap[:, 2, t0:t0 + 2], AF.Ln, [sc.b, epsc.b], [sc.b], bias=epsc.ap)
                    act(sc.ap[:, 2, t0:t0 + 2], sc.ap[:, 2, t0:t0 + 2], AF.Exp, [sc.b], [sc.b], scale=-0.5)
                return f

            def mk_s4(t0):
                def f():
                    for t in (t0, t0 + 1):
                        g_ = mg[t % 2]
                        q0 = b * TB + t * 128
                        stt(g_.ap, g_.ap, sc.ap[:, 2, t:t + 1], G1b.ap, ALU.mult, ALU.mult, [g_.b, sc.b, G1b.b], [g_.b])
                        tt(xt[t].ap, g_.ap, xt[t].ap, ALU.add, [g_.b, xt[t].b], [xt[t].b])
                        dma("pool", y_rows(q0, 128), xt[t].ap, [xt[t].b], [b_spill[st_]], "d_sp%d" % t)
                return f

            def mk_s5(t0):
                def f():
                    for t in (t0, t0 + 1):
                        act(junks[t % 2].ap, xt[t].ap, AF.Square, [xt[t].b], [junks[t % 2].b, sc.b],
                            scale=float(D) ** -0.5, accum_out=sc.ap[:, 3, t:t + 1])
                    act(sc.ap[:, 3, t0:t0 + 2], sc.ap[:, 3, t0:t0 + 2], AF.Ln, [sc.b, epsc.b], [sc.b], bias=epsc.ap)
                    act(sc.ap[:, 3, t0:t0 + 2], sc.ap[:, 3, t0:t0 + 2], AF.Exp, [sc.b], [sc.b], scale=-0.5)
                    for t in (t0, t0 + 1):
                        act(xn[t % 2].ap, xt[t].ap, AF.Copy, [xt[t].b, sc.b], [xn[t % 2].b], scale=sc.ap[:, 3, t:t + 1])
                return f

            def mk_tr(t):
                def f():
                    tp = tps[t % 2]
                    tpv = tp.ap.bitcast(BF16).rearrange("p (c n) -> p c n", c=8)
                    for c in range(8):
                        tr(tpv[:, c, :], xn[t % 2].ap[:, c * 128:(c + 1) * 128], idb.ap, [xn[t % 2].b, idb.b], [tp.b])
                    for c in range(8):
                        ts(h2.ap[:, c, t * 128:(t + 1) * 128], tpv[:, c, :], A2c.ap[:, c:c + 1], B2c[:, c:c + 1],
                           ALU.mult, ALU.add, [tp.b, A2c.b, modc.b], [h2.b])
                return f

            for t0 in (0, 2):
                for t in (t0, t0 + 1):
                    for half in range(2):
                        stages.append(mk_combo(t, half))
                stages.append(mk_s3(t0))
                stages.append(mk_s4(t0))
                stages.append(mk_s5(t0))
                stages.append(None)
                stages.append(mk_tr(t0))
                stages.append(mk_tr(t0 + 1))
            return stages

        def d2_unit(b, j, u):
            h2 = h2T[b % 2]
            r = u % NRW
            dma("sp", wgu[r].ap, s_wgu.ap[j], [s_wgu.b], [wgu[r].b], "d_wgu%d" % r)
            gb, ub = banks[3 + 2 * (u % 2)], banks[4 + 2 * (u % 2)]
            for c in range(8):
                mm(gb.ap, wgu[r].ap[:, 0, c, :], h2.ap[:, c, :], c == 0, c == 7, [wgu[r].b, h2.b], [gb.b])
            for c in range(8):
                mm(ub.ap, wgu[r].ap[:, 1, c, :], h2.ap[:, c, :], c == 0, c == 7, [wgu[r].b, h2.b], [ub.b])
            s_ = sg[u % 2]
            act(s_.ap, gb.ap, AF.Silu, [gb.b], [s_.b])
            tt(actT.ap[:, j, :], s_.ap, ub.ap, ALU.mult, [s_.b, ub.b], [actT.b])

        def d3(b):
            for t in range(4):
                q0 = b * TB + t * 128
                yi = state["yi"]
                f_ = fst[yi % 2]
                x1_ = x1b[yi % 2]
                dma("sp", x1_.ap, y_rows(q0, 128), [b_spill[b % 2]], [x1_.b], "d_x1b%d" % (yi % 2))
                for half in range(2):
                    hs = slice(half * 512, (half + 1) * 512)
                    fb = banks[1 + half]
                    for j in range(NFF):
                        mm(fb.ap, actT.ap[:, j, t * 128:(t + 1) * 128], wd.ap[:, j, hs], j == 0, j == NFF - 1,
                           [actT.b, wd.b], [fb.b])
                    act(f_.ap[:, hs], fb.ap, AF.Copy, [fb.b], [f_.b])
                k = yi % 4
                act(junk.ap, f_.ap, AF.Square, [f_.b], [junk.b, sf.b], scale=float(D) ** -0.5, accum_out=sf.ap[:, k:k + 1])
                act(sf.ap[:, k:k + 1], sf.ap[:, k:k + 1], AF.Ln, [sf.b, epsc.b], [sf.b], bias=epsc.ap)
                act(sf.ap[:, k:k + 1], sf.ap[:, k:k + 1], AF.Exp, [sf.b], [sf.b], scale=-0.5)
                stt(f_.ap, f_.ap, sf.ap[:, k:k + 1], G2b.ap, ALU.mult, ALU.mult, [f_.b, sf.b, G2b.b], [f_.b])
                tt(f_.ap, f_.ap, x1_.ap, ALU.add, [f_.b, x1_.b], [f_.b])
                dma("pool", y_rows(q0, 128), f_.ap, [f_.b, x1_.b], [b_yout], "d_y%d" % (yi % 2), is_out=True)
                state["yi"] = yi + 1

        for f in d1_stages(0):
            if f:
                f()
        u = 0
        for b in range(NB):
            stages = d1_stages(b + 1) if b + 1 < NB else []
            for j in range(NFF):
                d2_unit(b, j, u)
                u += 1
                if stages:
                    f = stages.pop(0)
                    if f:
                        f()
            while stages:
                f = stages.pop(0)
                if f:
                    f()
            d3(b)
        P.barrier()
        A.release(m)

    m0 = A.mark()
    dma("sp", gcols.ap, gcols_d, [], [gcols.b], "gcols")
    dma("sp", badac.ap, badac_d, [], [badac.b], "badac")
    mod_both()
    init()
    P.barrier()
    A.release(m0)
    phase_a(lambda r0, n: xs_d[r0:r0 + n, :], SS)
    init_b()
    phase_b(lambda r0, n: xsq_d[8 + r0:8 + r0 + n, :], (xsq_d[0:8, :], xsq_d[8 + QJ:16 + QJ, :]), 0)
    phase_c(SS, cosq_d, sinq_d)
    phase_d(lambda r0, n: xsq_d[8 + r0:8 + r0 + n, :], lambda r0, n: ys_d[r0:r0 + n, :])
    load_mod1()
    phase_a(lambda r0, n: xp_d[8 + r0:8 + r0 + n, :], SP)
    for j in range(NJP):
        j0 = j * QJ
        phase_b(lambda r0, n, j0=j0: xp_d[8 + j0 + r0:8 + j0 + r0 + n, :],
                (xp_d[j0:j0 + 8, :], xp_d[8 + j0 + QJ:16 + j0 + QJ, :]), 1 + j)
        phase_c(SP, cosk_d[:, j0:j0 + QJ], sink_d[:, j0:j0 + QJ])
        phase_d(lambda r0, n, j0=j0: xp_d[8 + j0 + r0:8 + j0 + r0 + n, :],
                lambda r0, n, j0=j0: yp_d[j0 + r0:j0 + r0 + n, :])

    P.finalize(nc, st)
    with nc.Block() as block:
        @block.tensor
        def _(e):
            P.emit("pe", e)

        @block.scalar
        def _(e):
            P.emit("act", e)

        @block.vector
        def _(e):
            P.emit("dve", e)

        @block.gpsimd
        def _(e):
            P.emit("pool", e)

        @block.sync
        def _(e):
            P.emit("sp", e)
            P.final_waits(e)
    st.close()
    build.stats = {k: len(v) for k, v in P.q.items()}
    build.peak = A.peak
    return nc


def rope_tables(S):
    inv_freq = (np.float32(10000.0) ** (-(np.arange(0, 64, 2, dtype=np.float32)) / np.float32(64))).astype(np.float32)
    ang = (np.arange(S, dtype=np.float32)[:, None] * inv_freq[None, :]).astype(np.float32)
    cos = np.cos(ang).astype(np.float32).T
    sin = np.sin(ang).astype(np.float32).T
    cos2 = np.concatenate([cos, cos], 0)
    sins = np.concatenate([-sin, sin], 0)
    return np.ascontiguousarray(cos2), np.ascontiguousarray(sins)


def pool_tables(S, start, n):
    t = np.arange(start, start + n)
    invc = np.zeros((128, 2, n), np.float32)
    for g, w in enumerate((2, 4, 8, 16)):
        lo = np.clip(t - w // 2, 0, S)
        hi = np.clip(t + w - w // 2, 0, S)
        v = (1.0 / (hi - lo)).astype(np.float32)
        c, half = g // 2, g % 2
        invc[half * 64:(half + 1) * 64, c, :] = v[None, :]
    hp = np.concatenate([np.arange(start - 8, start), np.arange(start + n, start + n + 8)])
    hm = ((hp >= 0) & (hp < S)).astype(np.float32)
    return invc, np.ascontiguousarray(np.broadcast_to(hm[None, :], (128, 16)))


def col_layout(v):
    return np.ascontiguousarray(v.reshape(-1, 128).T)


_NC_CACHE = {}


def make_in_maps(cfg, inp):
    SP, SS, QJ, NJP, SMAX = cfg.SP, cfg.SS, cfg.QJ, cfg.NJP, cfg.SMAX
    f = lambda a: np.ascontiguousarray(np.asarray(a, dtype=np.float32))
    w_in = f(inp["w_in"][0])
    w_in_x = np.concatenate([w_in, w_in[:, 928:960], w_in[:, 896:928]], axis=1)
    w_uq = f(inp["w_uq"][0])
    sw = []
    for h in range(NH):
        b = h * 192 + 128
        sw += [w_uq[:, b + 32:b + 64], w_uq[:, b:b + 32]]
    w_uq_x = np.concatenate([w_uq] + sw, axis=1)
    pool_w = f(inp["pool_w"][0])
    pw_bd = np.zeros((2, 128, 128), np.float32)
    for g in range(4):
        c, hf = g // 2, g % 2
        pw_bd[c, hf * 64:(hf + 1) * 64, hf * 64:(hf + 1) * 64] = pool_w[g]
    gcols = np.concatenate([
        col_layout(f(inp["g_mix_pre"][0])), col_layout(f(inp["g_ffn_pre"][0])), col_layout(f(inp["g_q_a"][0])),
        col_layout(f(inp["g_kv_a"][0])), col_layout(f(inp["pool_scale"][0])),
        col_layout(np.concatenate([f(inp["g_pool_out"][0]), f(inp["g_attn_out"][0])]))], axis=1)
    gpost = np.stack([f(inp["g_mix_post"][0]), f(inp["g_ffn_post"][0])])
    b_ada = f(inp["b_ada"][0])
    cosk, sink = rope_tables(SMAX)
    xs = f(inp["x_sample"][0])
    xs_pad = np.concatenate([np.zeros((8, D), np.float32), xs, np.zeros((8, D), np.float32)], 0)
    shared = {
        "xs": xs, "w_ada": f(inp["w_ada"][0]), "b_ada_col": col_layout(b_ada), "b_ada_row": b_ada,
        "gcols": np.ascontiguousarray(gcols), "gpost": np.ascontiguousarray(gpost), "w_in_x": np.ascontiguousarray(w_in_x),
        "w_uq_x": np.ascontiguousarray(w_uq_x), "w_ukv": f(inp["w_ukv"][0]), "w_out": f(inp["w_out"][0]),
        "w_gate": f(inp["w_gate"][0]), "w_up": f(inp["w_up"][0]), "w_down": f(inp["w_down"][0]), "pw_bd": pw_bd,
        "cosk": cosk, "sink": sink, "ident": np.eye(128, dtype=np.float32),
    }
    ptabs = [pool_tables(SP, j * QJ, QJ) for j in range(NJP)]
    maps = []
    for c in range(NCORES):
        xp = f(inp["x_prompt"][c])
        st_ = pool_tables(SS, c * QJ, QJ)
        m = dict(shared)
        m["xp"] = np.concatenate([np.zeros((8, D), np.float32), xp, np.zeros((8, D), np.float32)], 0)
        m["xsq"] = np.ascontiguousarray(xs_pad[c * QJ:c * QJ + QJ + 16])
        m["cvec"] = np.stack([col_layout(f(inp["c_sample"][0])), col_layout(f(inp["c_prompt"][c]))])
        m["cosq"] = np.ascontiguousarray(cosk[:, c * QJ:(c + 1) * QJ])
        m["sinq"] = np.ascontiguousarray(sink[:, c * QJ:(c + 1) * QJ])
        m["invc"] = np.stack([st_[0]] + [p[0] for p in ptabs])
        m["hmask"] = np.stack([st_[1]] + [p[1] for p in ptabs])
        maps.append(m)
    return maps


def run(cfg, inp, trace=False):
    key = (cfg.SP, cfg.SS, cfg.NQB)
    if key not in _NC_CACHE:
        _NC_CACHE[key] = build(cfg)
    nc = _NC_CACHE[key]
    maps = make_in_maps(cfg, inp)
    res = run_bass_kernel_spmd(nc, maps, core_ids=list(range(NCORES)), trace=trace)
    yp = np.stack([res.results[c]["yp"] for c in range(NCORES)], 0)
    ys = np.concatenate([res.results[c]["ys"] for c in range(NCORES)], 0)[None]
    return (yp.astype(np.float32), ys.astype(np.float32)), res


def kernel(**inputs):
    cfg = Cfg()
    out, _ = run(cfg, inputs)
    return out
```

```python
import os
import numpy as np
from contextlib import ExitStack
import concourse.bass as bass
import concourse.mybir as mybir
from concourse.bass_utils import run_bass_kernel_spmd

F32 = mybir.dt.float32
BF16 = mybir.dt.bfloat16
AF = mybir.ActivationFunctionType
ALU = mybir.AluOpType

D = 1024
DFF = 2816
NFF = 22
NH = 6
EPS = 1e-6
QSCALE = 192.0 ** -0.5
NCORES = 8
ENGS = ("pe", "act", "dve", "pool", "sp")
STRICT_SAME_ENGINE = os.environ.get("KSTRICT") == "1"


class Buf:
    __slots__ = ("name", "writers", "readers", "war")

    def __init__(self, name):
        self.name = name
        self.writers = []
        self.readers = []
        self.war = []


class T:
    __slots__ = ("ap", "b")

    def __init__(self, ap, name):
        self.ap = ap
        self.b = Buf(name)


class Op:
    __slots__ = ("eng", "fn", "deps", "needs_inc", "ticket", "dma", "semkey", "cpos", "barrier")

    def __init__(self, eng, fn, dma, semkey):
        self.eng = eng
        self.fn = fn
        self.deps = []
        self.needs_inc = False
        self.ticket = 0
        self.dma = dma
        self.semkey = semkey
        self.cpos = 0
        self.barrier = False


class Prog:
    def __init__(self):
        self.q = {e: [] for e in ENGS}
        self.ncomp = {e: 0 for e in ENGS}
        self.dma_count = {}
        self.last_dma = {}
        self.last_comp = {}
        self.out_ops = []
        self.nobarrier = set(("s_wd", "s_wg", "s_wu"))

    def op(self, eng, fn, reads=(), writes=(), dma=False, semkey=None, is_out=False):
        o = Op(eng, fn, dma, semkey)
        deps = set()
        for b in reads:
            for w in b.writers:
                deps.add(w)
        for b in writes:
            if b.readers:
                for r in b.readers:
                    deps.add(r)
                for w in b.writers:
                    deps.add(w)
            else:
                for r in b.war:
                    deps.add(r)
        deps.discard(o)
        o.deps = list(deps)
        for b in writes:
            if b.readers:
                b.war = b.readers
                b.readers = []
                b.writers = [o]
            else:
                b.writers.append(o)
        for b in reads:
            if b not in writes:
                b.readers.append(o)
        if not dma:
            self.ncomp[eng] += 1
            self.last_comp[eng] = o
        o.cpos = self.ncomp[eng]
        self.q[eng].append(o)
        if dma:
            c = self.dma_count.get(semkey, 0) + 16
            self.dma_count[semkey] = c
            o.ticket = c
            o.needs_inc = True
            self.last_dma[semkey] = o
        if is_out:
            self.out_ops.append(o)
        return o

    def barrier(self):
        front = list(self.last_comp.values()) + [o for k, o in self.last_dma.items() if k not in self.nobarrier]
        for e in ENGS:
            o = Op(e, None, False, None)
            o.barrier = True
            o.deps = [x for x in front if x.dma or x.eng != e]
            o.cpos = self.ncomp[e]
            self.q[e].append(o)

    @staticmethod
    def _need_sem(x, y):
        if x.dma:
            return True
        if x.eng == y.eng:
            if x.eng == "pe":
                return False
            if y.dma:
                return True
            if x.eng in ("act", "dve"):
                return STRICT_SAME_ENGINE or (y.cpos - x.cpos) < 3
            return True
        return True

    def finalize(self, nc, stack):
        for e in ENGS:
            for y in self.q[e]:
                y.deps = [x for x in y.deps if self._need_sem(x, y)]
                for x in y.deps:
                    if not x.dma:
                        x.needs_inc = True
        for e in ENGS:
            t = 0
            for o in self.q[e]:
                if not o.dma and o.needs_inc:
                    t += 1
                    o.ticket = t
        self.esem = {e: stack.enter_context(nc.semaphore("s_" + e)) for e in ENGS}
        self.dsem = {k: stack.enter_context(nc.semaphore("d_%d" % i)) for i, k in enumerate(self.dma_count)}

    def emit(self, eng_name, eng):
        waited = {}
        for o in self.q[eng_name]:
            need = {}
            for x in o.deps:
                s = self.dsem[x.semkey] if x.dma else self.esem[x.eng]
                k = id(s)
                if k not in need or need[k][1] < x.ticket:
                    need[k] = (s, x.ticket)
            for k, (s, v) in need.items():
                if waited.get(k, 0) >= v:
                    continue
                eng.wait_ge(s, v)
                waited[k] = v
            if o.fn is None:
                continue
            ins = o.fn(eng)
            if o.needs_inc:
                if o.dma:
                    ins.then_inc(self.dsem[o.semkey], 16)
                else:
                    ins.then_inc(self.esem[o.eng], 1)

    def final_waits(self, eng):
        done = {}
        for o in self.out_ops:
            done[o.semkey] = max(done.get(o.semkey, 0), o.ticket)
        for k, v in done.items():
            eng.wait_ge(self.dsem[k], v)


class Arena:
    def __init__(self, tensor, nbytes):
        self.t = tensor
        self.cap = nbytes
        self.off = 0
        self.peak = 0

    def alloc(self, name, shape, dtype):
        esz = 4 if dtype == F32 else 2
        n = 1
        for s in shape:
            n *= s
        nb = n * esz
        off = (self.off + 63) // 64 * 64
        assert off + nb <= self.cap, "arena overflow at %s: %d + %d > %d" % (name, off, nb, self.cap)
        self.off = off + nb
        self.peak = max(self.peak, self.off)
        v = self.t[:, off // 2:(off + nb) // 2]
        if dtype == F32:
            v = v.bitcast(F32)
        if len(shape) == 2:
            v = v.rearrange("p (a b) -> p a b", a=shape[0])
        elif len(shape) == 3:
            v = v.rearrange("p (a b c) -> p a b c", a=shape[0], b=shape[1])
        return T(v, name)

    def mark(self):
        return self.off

    def release(self, m):
        self.off = m


class Cfg:
    def __init__(self, SP=8192, SS=16384, NQB=4):
        self.SP, self.SS, self.NQB = SP, SS, NQB
        self.QJ = 512 * NQB
        self.NJP = SP // self.QJ
        assert SS // NCORES == self.QJ and SP % self.QJ == 0
        self.SMAX = max(SP, SS)


def build(cfg):
    SP, SS, NQB, QJ, NJP, SMAX = cfg.SP, cfg.SS, cfg.NQB, cfg.QJ, cfg.NJP, cfg.SMAX
    nc = bass.Bass("TRN2", target_bir_lowering=False)

    def din(name, shape):
        return nc.dram_tensor(name, list(shape), F32, kind="ExternalInput").ap()

    xp_d = din("xp", (SP + 16, D))
    xs_d = din("xs", (SS, D))
    xsq_d = din("xsq", (QJ + 16, D))
    cvec_d = din("cvec", (2, 128, 8))
    wada_d = din("w_ada", (D, 6 * D))
    badac_d = din("b_ada_col", (128, 48))
    badar_d = din("b_ada_row", (6 * D,))
    gcols_d = din("gcols", (128, 31))
    gpost_d = din("gpost", (2, D))
    win_d = din("w_in_x", (D, 1024))
    wuq_d = din("w_uq_x", (384, 1536))
    wukv_d = din("w_ukv", (256, 1536))
    wout_d = din("w_out", (D, D))
    wg_d = din("w_gate", (D, DFF))
    wu_d = din("w_up", (D, DFF))
    wd_d = din("w_down", (DFF, D))
    pwbd_d = din("pw_bd", (2, 128, 128))
    cosk_d = din("cosk", (64, SMAX))
    sink_d = din("sink", (64, SMAX))
    cosq_d = din("cosq", (64, QJ))
    sinq_d = din("sinq", (64, QJ))
    invc_d = din("invc", (1 + NJP, 128, 2, QJ))
    hmask_d = din("hmask", (1 + NJP, 128, 16))
    ident_d = din("ident", (128, 128))
    yp_d = nc.dram_tensor("yp", [SP, D], F32, kind="ExternalOutput").ap()
    ys_d = nc.dram_tensor("ys", [QJ, D], F32, kind="ExternalOutput").ap()

    def dscr(name, shape):
        return T(nc.dram_tensor(name, list(shape), BF16).ap(), name)

    s_win = dscr("s_win", (128, 8, 1024))
    s_wuq = dscr("s_wuq", (128, 3, 1536))
    s_wukv = dscr("s_wukv", (128, 2, 1536))
    s_wout = dscr("s_wout", (128, 8, 1024))
    s_wd = dscr("s_wd", (128, NFF, 1024))
    s_wgu = dscr("s_wgu", (NFF, 128, 2, 8, 128))
    lat_ckv = dscr("lat_ckv", (128, 2, SMAX))
    lat_kr = dscr("lat_kr", (64, SMAX))
    b_yout = Buf("yout")

    P = Prog()
    st = ExitStack()
    ARENA_BYTES = 207 * 1024
    arena_t = st.enter_context(nc.sbuf_tensor("arena", [128, ARENA_BYTES // 2], BF16))
    A = Arena(arena_t, ARENA_BYTES)
    banks = [T(st.enter_context(nc.psum_tensor("bk%d" % i, [128, 512], F32))[:], "bk%d" % i) for i in range(8)]

    def mm(out, lhsT, rhs, start, stop, R, W):
        P.op("pe", lambda e: e.matmul(out, lhsT=lhsT, rhs=rhs, start=start, stop=stop), reads=R, writes=W)

    def tr(out, in_, ident, R, W):
        P.op("pe", lambda e: e.transpose(out=out, in_=in_, identity=ident), reads=R, writes=W)

    def act(out, in_, func, R, W, **kw):
        P.op("act", lambda e: e.activation(out=out, in_=in_, func=func, **kw), reads=R, writes=W)

    def tt(out, in0, in1, op, R, W, eng="dve"):
        P.op(eng, lambda e: e.tensor_tensor(out=out, in0=in0, in1=in1, op=op), reads=R, writes=W)

    def ts(out, in0, s1, s2, op0, op1, R, W, eng="dve"):
        if op1 is None:
            P.op(eng, lambda e: e.tensor_scalar(out=out, in0=in0, scalar1=s1, scalar2=None, op0=op0),
                 reads=R, writes=W)
        else:
            P.op(eng, lambda e: e.tensor_scalar(out=out, in0=in0, scalar1=s1, scalar2=s2, op0=op0, op1=op1),
                 reads=R, writes=W)

    def stt(out, in0, scalar, in1, op0, op1, R, W, eng="dve"):
        P.op(eng, lambda e: e.scalar_tensor_tensor(out=out, in0=in0, scalar=scalar, in1=in1, op0=op0, op1=op1),
             reads=R, writes=W)

    def cp(out, in_, R, W, eng="dve"):
        P.op(eng, lambda e: e.tensor_copy(out=out, in_=in_), reads=R, writes=W)

    def memset(ap, val, W, eng="dve"):
        P.op(eng, lambda e: e.memset(ap, val), writes=W)

    def dma(eng, out, in_, R, W, key, is_out=False):
        P.op(eng, lambda e: e.dma_start(out=out, in_=in_), reads=R, writes=W, dma=True, semkey=key, is_out=is_out)

    idb = A.alloc("idb", (128,), BF16)
    ones_b = A.alloc("ones_b", (128,), BF16)
    ones_f = A.alloc("ones_f", (128,), F32)
    epsc = A.alloc("epsc", (1,), F32)
    gcols = A.alloc("gcols", (31,), F32)
    badac = A.alloc("badac", (48,), F32)
    modc = A.alloc("modc", (32,), F32)
    A1c = A.alloc("A1c", (8,), F32)
    A2c = A.alloc("A2c", (8,), F32)
    G1b = A.alloc("G1b", (D,), F32)
    G2b = A.alloc("G2b", (D,), F32)
    pwbd = A.alloc("pwbd", (2, 128), BF16)
    poolT = A.alloc("poolT", (2, QJ), BF16)
    attnT = A.alloc("attnT", (NH, QJ), BF16)
    G_Q, G_KV, G_PS, G_OUT = 16, 19, 21, 23
    CQ_MARK = A.mark()
    cqT = A.alloc("cqT", (3, QJ), BF16)
    PERSIST = A.mark()

    def init():
        idf = A.alloc("idf", (128,), F32)
        dma("sp", idf.ap, ident_d, [], [idf.b], "idf")
        cp(idb.ap, idf.ap, [idf.b], [idb.b])
        memset(ones_b.ap, 1.0, [ones_b.b])
        memset(ones_f.ap, 1.0, [ones_f.b])
        memset(epsc.ap, EPS, [epsc.b])
        dma("pool", pwbd.ap, pwbd_d.rearrange("c p m -> p c m"), [], [pwbd.b], "pwbd")
        dma("pool", s_win.ap, win_d.rearrange("(c p) m -> p c m", p=128), [], [s_win.b], "s_win")
        dma("pool", s_wuq.ap, wuq_d.rearrange("(c p) m -> p c m", p=128), [], [s_wuq.b], "s_wuq")
        dma("pool", s_wukv.ap, wukv_d.rearrange("(c p) m -> p c m", p=128), [], [s_wukv.b], "s_wukv")
        stf = [A.alloc("stf%d" % i, (D,), F32) for i in range(2)]
        stb = [A.alloc("stb%d" % i, (D,), BF16) for i in range(2)]
        for c in range(8):
            i = c % 2
            dma("sp", stf[i].ap, wout_d[c * 128:(c + 1) * 128, :], [], [stf[i].b], "stf%d" % i)
            ts(stb[i].ap, stf[i].ap, gcols.ap[:, G_OUT + c:G_OUT + c + 1], None, ALU.mult, None,
               [stf[i].b, gcols.b], [stb[i].b])
            dma("sp", s_wout.ap[:, c, :], stb[i].ap, [stb[i].b], [s_wout.b], "stb%d" % i)

    def init_b():
        dma("pool", s_wd.ap, wd_d.rearrange("(j p) n -> p j n", p=128), [], [s_wd.b], "s_wd")
        for j in range(NFF):
            dma("pool", s_wgu.ap[j, :, 0], wg_d[:, j * 128:(j + 1) * 128].rearrange("(c p) m -> p c m", p=128),
                [], [s_wgu.b], "s_wg")
            dma("pool", s_wgu.ap[j, :, 1], wu_d[:, j * 128:(j + 1) * 128].rearrange("(c p) m -> p c m", p=128),
                [], [s_wgu.b], "s_wu")

    s_modc = T(nc.dram_tensor("s_modc", [128, 48], F32).ap(), "s_modc")
    s_G = T(nc.dram_tensor("s_G", [128, 2, D], F32).ap(), "s_G")

    def mod_both():
        wad = [A.alloc("wad%d" % i, (8, 512), BF16) for i in range(12)]
        order = [(mi, half) for mi in (0, 1, 3, 4) for half in range(2)] + [(mi, half) for mi in (2, 5) for half in range(2)]
        for nb, (mi, half) in enumerate(order):
            col0 = mi * D + half * 512
            dma("pool", wad[nb].ap, wada_d[:, col0:col0 + 512].rearrange("(c p) m -> p c m", p=128),
                [], [wad[nb].b], "wad%d" % nb)
        rowb = [A.alloc("rowb%d" % i, (512,), F32) for i in range(4)]
        rowg = [A.alloc("rowg%d" % i, (512,), F32) for i in range(4)]
        for k, (mi, half) in enumerate(order[8:]):
            col0 = mi * D + half * 512
            dma("sp", rowb[k].ap, badar_d[col0:col0 + 512].partition_broadcast(128), [], [rowb[k].b], "rowb%d" % k)
            dma("sp", rowg[k].ap, gpost_d[k // 2, half * 512:(half + 1) * 512].partition_broadcast(128),
                [], [rowg[k].b], "rowg%d" % k)
        tmpc = A.alloc("m_tmpc", (48,), F32)
        tmpG = A.alloc("m_tmpG", (2, D), F32)
        rsum = A.alloc("m_rsum", (512,), F32)
        for si in range(2):
            cc = A.alloc("cc%d" % si, (8,), F32)
            csb = A.alloc("csb%d" % si, (8,), BF16)
            lhsb = A.alloc("lhsb%d" % si, (8, 128), BF16)
            dma("sp", cc.ap, cvec_d[si], [], [cc.b], "cc%d" % si)
            act(csb.ap, cc.ap, AF.Silu, [cc.b], [csb.b])
            for c in range(8):
                cp(lhsb.ap[:, c, :], csb.ap[:, c:c + 1].to_broadcast([128, 128]), [csb.b], [lhsb.b])
            pcol = banks[si]
            colmods = (0, 1, 3, 4)
            for nb in range(8):
                ci, half = nb // 2, nb % 2
                w = wad[nb]
                for m4 in range(4):
                    idx = ci * 8 + half * 4 + m4
                    for c in range(8):
                        mm(pcol.ap[:, idx:idx + 1], w.ap[:, c, m4 * 128:(m4 + 1) * 128], csb.ap[:, c:c + 1],
                           c == 0, c == 7, [w.b, csb.b], [pcol.b])
            if si == 0:
                mc, a1, a2, mcb, a1b, a2b = modc.ap, A1c.ap, A2c.ap, modc.b, A1c.b, A2c.b
            else:
                mc, a1, a2 = tmpc.ap[:, 0:32], tmpc.ap[:, 32:40], tmpc.ap[:, 40:48]
                mcb = a1b = a2b = tmpc.b
            for ci, mi in enumerate(colmods):
                tt(mc[:, ci * 8:(ci + 1) * 8], pcol.ap[:, ci * 8:(ci + 1) * 8], badac.ap[:, mi * 8:(mi + 1) * 8],
                   ALU.add, [pcol.b, badac.b], [mcb])
            stt(a1, mc[:, 8:16], 1.0, gcols.ap[:, 0:8], ALU.add, ALU.mult, [mcb, gcols.b], [a1b])
            stt(a2, mc[:, 24:32], 1.0, gcols.ap[:, 8:16], ALU.add, ALU.mult, [mcb, gcols.b], [a2b])
            for k in range(4):
                gi, half = k // 2, k % 2
                w = wad[8 + k]
                pr = banks[2 + (k % 2) + 2 * si]
                for c in range(8):
                    mm(pr.ap, lhsb.ap[:, c, :], w.ap[:, c, :], c == 0, c == 7, [lhsb.b, w.b], [pr.b])
                tt(rsum.ap, pr.ap, rowb[k].ap, ALU.add, [pr.b, rowb[k].b], [rsum.b])
                if si == 0:
                    G = (G1b, G2b)[gi]
                    tt(G.ap[:, half * 512:(half + 1) * 512], rsum.ap, rowg[k].ap, ALU.mult, [rsum.b, rowg[k].b], [G.b])
                else:
                    tt(tmpG.ap[:, gi, half * 512:(half + 1) * 512], rsum.ap, rowg[k].ap, ALU.mult,
                       [rsum.b, rowg[k].b], [tmpG.b])
            if si == 1:
                dma("sp", s_modc.ap, tmpc.ap, [tmpc.b], [s_modc.b], "m_oc")
                dma("sp", s_G.ap, tmpG.ap, [tmpG.b], [s_G.b], "m_oG")

    def load_mod1():
        dma("sp", modc.ap, s_modc.ap[:, 0:32], [s_modc.b], [modc.b], "l_modc")
        dma("sp", A1c.ap, s_modc.ap[:, 32:40], [s_modc.b], [A1c.b], "l_a1")
        dma("sp", A2c.ap, s_modc.ap[:, 40:48], [s_modc.b], [A2c.b], "l_a2")
        dma("sp", G1b.ap, s_G.ap[:, 0, :], [s_G.b], [G1b.b], "l_g1")
        dma("sp", G2b.ap, s_G.ap[:, 1, :], [s_G.b], [G2b.b], "l_g2")

    B1c = modc.ap[:, 0:8]
    B2c = modc.ap[:, 16:24]

    class Front:
        def __init__(self, tag, nx=8, nxn=8):
            self.xt = [A.alloc("%s_xt%d" % (tag, i), (D,), F32) for i in range(nx)]
            self.junk = [A.alloc("%s_junk%d" % (tag, i), (D,), BF16) for i in range(4)]
            self.xn = [A.alloc("%s_xn%d" % (tag, i), (D,), BF16) for i in range(nxn)]
            self.ss = [A.alloc("%s_ss%d" % (tag, i), (4,), F32) for i in range(2)]
            self.i = 0
            self.g = 0
            self.tag = tag

        def part1(self, tiles, n):
            k = len(tiles)
            ss = self.ss[self.g % 2]
            xns = [self.xn[(self.g % 2) * 4 + t] for t in range(k)] if len(self.xn) >= 8 else [self.xn[t] for t in range(k)]
            self.g += 1
            xts = []
            for srcs in tiles:
                xi = self.i % len(self.xt)
                self.i += 1
                xt = self.xt[xi]
                xts.append(xt)
                for (ap, r0, nr) in srcs:
                    dma("sp", xt.ap[r0:r0 + nr, :], ap, [], [xt.b], "%s_xt%d" % (self.tag, xi))
            for t, xt in enumerate(xts):
                act(self.junk[t].ap[0:n, :], xt.ap[0:n, :], AF.Square, [xt.b], [self.junk[t].b, ss.b],
                    scale=float(D) ** -0.5, accum_out=ss.ap[0:n, t:t + 1])
            act(ss.ap[0:n, 0:k], ss.ap[0:n, 0:k], AF.Ln, [ss.b, epsc.b], [ss.b], bias=epsc.ap[0:n, :])
            act(ss.ap[0:n, 0:k], ss.ap[0:n, 0:k], AF.Exp, [ss.b], [ss.b], scale=-0.5)
            for t, xt in enumerate(xts):
                act(xns[t].ap[0:n, :], xt.ap[0:n, :], AF.Copy, [xt.b, ss.b], [xns[t].b], scale=ss.ap[0:n, t:t + 1])
            return xns

        def part2_tile(self, xns, t, n, Acol, Bcol, ABb, hT, tps):
            xn = xns[t]
            tp = tps[t % len(tps)]
            tpv = tp.ap.bitcast(BF16).rearrange("p (c n) -> p c n", c=8)
            for c in range(8):
                tr(tpv[:, c, 0:n], xn.ap[0:n, c * 128:(c + 1) * 128], idb.ap[0:n, 0:n], [xn.b, idb.b], [tp.b])
            for c in range(8):
                ts(hT.ap[:, c, t * n:(t + 1) * n], tpv[:, c, 0:n], Acol[:, c:c + 1], Bcol[:, c:c + 1],
                   ALU.mult, ALU.add, [tp.b] + ABb, [hT.b])

        def part2(self, xns, n, Acol, Bcol, ABb, hT, tps):
            for t in range(len(xns)):
                self.part2_tile(xns, t, n, Acol, Bcol, ABb, hT, tps)

        def run_block(self, tiles, n, Acol, Bcol, ABb, hT, tps):
            xns = self.part1(tiles, n)
            self.part2(xns, n, Acol, Bcol, ABb, hT, tps)

    def rms_sq(pss, nchunk, sq):
        for c in range(nchunk):
            act(sq.ap[:, c, :], pss[c].ap, AF.Square, [pss[c].b], [sq.b])

    def rms_fin(nchunk, inv_n, sq, ssb, rstdb):
        for c in range(nchunk):
            mm(ssb.ap, ones_b.ap, sq.ap[:, c, :], c == 0, c == nchunk - 1, [ones_b.b, sq.b], [ssb.b])
        act(rstdb.ap, ssb.ap, AF.Ln, [ssb.b, epsc.b], [rstdb.b], scale=inv_n, bias=epsc.ap)
        act(rstdb.ap, rstdb.ap, AF.Exp, [rstdb.b], [rstdb.b], scale=-0.5)

    def rms_bcast(pss, nchunk, inv_n, sq, ssb, rstdb):
        rms_sq(pss, nchunk, sq)
        rms_fin(nchunk, inv_n, sq, ssb, rstdb)

    def phase_a(x_rows, S):
        m = A.mark()
        win = A.alloc("a_win", (8, 384), BF16)
        dma("sp", win.ap, s_win.ap[:, :, 640:1024], [s_win.b], [win.b], "a_win")
        fr = Front("a")
        hT = [A.alloc("a_hT%d" % i, (8, 512), BF16) for i in range(2)]
        sq = A.alloc("a_sq", (2, 512), BF16)
        rstdb = A.alloc("a_rstdb", (512,), F32)
        ckvn = [A.alloc("a_ckvn%d" % i, (2, 512), BF16) for i in range(2)]
        krot = [A.alloc("a_krot%d" % i, (512,), BF16) for i in range(2)]
        cst = [A.alloc("a_cos%d" % i, (512,), F32) for i in range(2)]
        snt = [A.alloc("a_sin%d" % i, (512,), F32) for i in range(2)]
        t1 = A.alloc("a_t1", (512,), F32)
        t2 = A.alloc("a_t2", (512,), F32)
        tps = [banks[0], banks[7]]
        b1, b2, b3, b4, b5 = banks[1], banks[2], banks[5], banks[6], banks[3]
        nblk = S // 512
        AB = [A1c.b, modc.b]

        def tiles_of(blk):
            return [[(x_rows(blk * 512 + t4 * 128, 128), 0, 128)] for t4 in range(4)]

        xq = {0: fr.part1(tiles_of(0), 128)}
        fr.part2(xq[0], 128, A1c.ap, B1c, AB, hT[0], tps)
        if nblk > 1:
            xq[1] = fr.part1(tiles_of(1), 128)
        for blk in range(nblk):
            h = hT[blk % 2]
            r = blk % 2
            dma("sp", cst[r].ap[0:64, :], cosk_d[:, blk * 512:(blk + 1) * 512], [], [cst[r].b], "a_cos%d" % r)
            dma("sp", snt[r].ap[0:64, :], sink_d[:, blk * 512:(blk + 1) * 512], [], [snt[r].b], "a_sin%d" % r)
            if blk + 2 < nblk:
                xq[blk + 2] = fr.part1(tiles_of(blk + 2), 128)
            for g, (c0, M, bk) in enumerate(((0, 128, b1), (128, 128, b2), (256, 64, b3), (320, 64, b4))):
                for c in range(8):
                    mm(bk.ap[0:M, :], win.ap[:, c, c0:c0 + M], h.ap[:, c, :], c == 0, c == 7, [win.b, h.b], [bk.b])
                if blk + 1 < nblk:
                    fr.part2_tile(xq[blk + 1], g, 128, A1c.ap, B1c, AB, hT[(blk + 1) % 2], tps)
            xq.pop(blk, None)
            rms_sq([b1, b2], 2, sq)
            rms_fin(2, 1.0 / 256, sq, b5, rstdb)
            for c, bk in enumerate((b1, b2)):
                stt(ckvn[r].ap[:, c, :], bk.ap, gcols.ap[:, G_KV + c:G_KV + c + 1], rstdb.ap, ALU.mult, ALU.mult,
                    [bk.b, gcols.b, rstdb.b], [ckvn[r].b])
            tt(t1.ap[0:64, :], b3.ap[0:64, :], cst[r].ap[0:64, :], ALU.mult, [b3.b, cst[r].b], [t1.b])
            tt(t2.ap[0:64, :], b4.ap[0:64, :], snt[r].ap[0:64, :], ALU.mult, [b4.b, snt[r].b], [t2.b])
            tt(krot[r].ap[0:64, :], t1.ap[0:64, :], t2.ap[0:64, :], ALU.add, [t1.b, t2.b], [krot[r].b])
            dma("pool", lat_ckv.ap[:, :, blk * 512:(blk + 1) * 512], ckvn[r].ap, [ckvn[r].b], [lat_ckv.b], "a_lo%d" % r)
            dma("pool", lat_kr.ap[:, blk * 512:(blk + 1) * 512], krot[r].ap[0:64, :], [krot[r].b], [lat_kr.b], "a_ko%d" % r)
        P.barrier()
        A.release(m)

    def phase_b(x_rows, halo_rows, jk):
        m = A.mark()
        win = A.alloc("b_win", (8, 640), BF16)
        dma("sp", win.ap, s_win.ap[:, :, 0:640], [s_win.b], [win.b], "b_win")
        fr = Front("b", nx=4, nxn=4)
        hT = [A.alloc("b_hT%d" % i, (8, 512), BF16) for i in range(2)]
        hTh = A.alloc("b_hTh", (8, 16), BF16)
        sq = A.alloc("b_sq", (3, 512), BF16)
        rstdb = A.alloc("b_rstdb", (512,), F32)
        N = QJ + 16
        uext = A.alloc("b_uext", (2, N), F32)
        sA = A.alloc("b_sA", (2, N), F32)
        sB = A.alloc("b_sB", (2, N), F32)
        invc = A.alloc("b_invc", (2, QJ), F32)
        hmask = A.alloc("b_hmask", (16,), F32)
        tmpw = A.alloc("b_tmpw", (QJ,), F32)
        pooledT = A.alloc("b_pooled", (2, QJ), BF16)
        dma("sp", invc.ap, invc_d[jk], [], [invc.b], "b_invc")
        dma("sp", hmask.ap, hmask_d[jk], [], [hmask.b], "b_hmask")
        tp = banks[0]
        ub = (banks[1], banks[2])
        qb_ = (banks[3], banks[4], banks[5])
        ssb = banks[6]
        b7 = banks[7]
        hl, hr = halo_rows
        fr.run_block([[(hl, 0, 8), (hr, 8, 8)]], 16, A1c.ap, B1c, [A1c.b, modc.b], hTh, [tp])

        AB = [A1c.b, modc.b]

        def tiles_of(blk):
            return [[(x_rows(blk * 512 + t4 * 128, 128), 0, 128)] for t4 in range(4)]

        xq = {0: fr.part1(tiles_of(0), 128)}
        fr.part2(xq[0], 128, A1c.ap, B1c, AB, hT[0], [tp])
        for cu in range(2):
            for c in range(8):
                mm(b7.ap[:, cu * 16:(cu + 1) * 16], win.ap[:, c, cu * 128:(cu + 1) * 128], hTh.ap[:, c, :],
                   c == 0, c == 7, [win.b, hTh.b], [b7.b])
        for cu in range(2):
            tt(uext.ap[:, cu, 0:8], b7.ap[:, cu * 16:cu * 16 + 8], hmask.ap[:, 0:8], ALU.mult,
               [b7.b, hmask.b], [uext.b])
            tt(uext.ap[:, cu, 8 + QJ:16 + QJ], b7.ap[:, cu * 16 + 8:cu * 16 + 16], hmask.ap[:, 8:16], ALU.mult,
               [b7.b, hmask.b], [uext.b])
        for blk in range(NQB):
            h = hT[blk % 2]
            if blk + 1 < NQB:
                xq[blk + 1] = fr.part1(tiles_of(blk + 1), 128)
            groups = [(ub[0], 0), (ub[1], 128), (qb_[0], 256), (qb_[1], 384), (qb_[2], 512)]
            for g, (bk, c0) in enumerate(groups):
                for c in range(8):
                    mm(bk.ap, win.ap[:, c, c0:c0 + 128], h.ap[:, c, :], c == 0, c == 7, [win.b, h.b], [bk.b])
                if g < 2:
                    act(uext.ap[:, g, 8 + blk * 512:8 + (blk + 1) * 512], bk.ap, AF.Copy, [bk.b], [uext.b])
                if blk + 1 < NQB and g < 4:
                    fr.part2_tile(xq[blk + 1], g, 128, A1c.ap, B1c, AB, hT[(blk + 1) % 2], [tp])
            xq.pop(blk, None)
            rms_bcast(list(qb_), 3, 1.0 / 384, sq, ssb, rstdb)
            for cq in range(3):
                stt(cqT.ap[:, cq, blk * 512:(blk + 1) * 512], qb_[cq].ap, gcols.ap[:, G_Q + cq:G_Q + cq + 1],
                    rstdb.ap, ALU.mult, ALU.mult, [qb_[cq].b, gcols.b, rstdb.b], [cqT.b])
        u = uext
        tt(sA.ap[:, :, 1:N], u.ap[:, :, 1:N], u.ap[:, :, 0:N - 1], ALU.add, [u.b], [sA.b])
        wins = []

        def pooled(src, rows, c, sh):
            tt(tmpw.ap[rows, :], src.ap[rows, c, sh:sh + QJ], invc.ap[rows, c, :], ALU.mult, [src.b, invc.b], [tmpw.b])
            tt(pooledT.ap[rows, c, :], tmpw.ap[rows, :], u.ap[rows, c, 8:8 + QJ], ALU.subtract,
               [tmpw.b, u.b], [pooledT.b])

        lo, hi = slice(0, 64), slice(64, 128)
        pooled(sA, lo, 0, 8)
        tt(sB.ap[:, :, 3:N], sA.ap[:, :, 3:N], sA.ap[:, :, 1:N - 2], ALU.add, [sA.b], [sB.b])
        pooled(sB, hi, 0, 9)
        tt(sA.ap[:, :, 7:N], sB.ap[:, :, 7:N], sB.ap[:, :, 3:N - 4], ALU.add, [sB.b], [sA.b])
        pooled(sA, lo, 1, 11)
        tt(sB.ap[:, :, 15:N], sA.ap[:, :, 15:N], sA.ap[:, :, 7:N - 8], ALU.add, [sA.b], [sB.b])
        pooled(sB, hi, 1, 15)
        for blk in range(NQB):
            for c in range(2):
                mm(b7.ap, pwbd.ap[:, c, :], pooledT.ap[:, c, blk * 512:(blk + 1) * 512], True, True,
                   [pwbd.b, pooledT.b], [b7.b])
                ts(poolT.ap[:, c, blk * 512:(blk + 1) * 512], b7.ap, gcols.ap[:, G_PS + c:G_PS + c + 1], None,
                   ALU.mult, None, [b7.b, gcols.b], [poolT.b])
        P.barrier()
        A.release(m)

    def phase_c(S, cos_src, sin_src):
        m = A.mark()
        wuq = A.alloc("c_wuq", (3, 1536), BF16)
        wukv = A.alloc("c_wukv", (2, 1536), BF16)
        dma("sp", wuq.ap, s_wuq.ap, [s_wuq.b], [wuq.b], "c_wuq")
        dma("sp", wukv.ap, s_wukv.ap, [s_wukv.b], [wukv.b], "c_wukv")
        cosq = A.alloc("c_cosq", (QJ,), F32)
        sinq = A.alloc("c_sinq", (QJ,), F32)
        dma("sp", cosq.ap[0:64, :], cos_src, [], [cosq.b], "c_cosq")
        dma("sp", sinq.ap[0:64, :], sin_src, [], [sinq.b], "c_sinq")
        ts(cosq.ap[0:64, :], cosq.ap[0:64, :], QSCALE, None, ALU.mult, None, [cosq.b], [cosq.b])
        ts(sinq.ap[0:64, :], sinq.ap[0:64, :], QSCALE, None, ALU.mult, None, [sinq.b], [sinq.b])
        Qn = [A.alloc("c_Qn%d" % i, (QJ,), BF16) for i in range(2)]
        Qr = [A.alloc("c_Qr%d" % i, (QJ,), BF16) for i in range(2)]
        for i in range(2):
            memset(Qr[i].ap[64:128, :], 0.0, [Qr[i].b])
        NR = 3
        ckb = [A.alloc("c_ckb%d" % i, (2, 512), BF16) for i in range(NR)]
        krb = [A.alloc("c_krb%d" % i, (512,), BF16) for i in range(NR)]
        for i in range(NR):
            memset(krb[i].ap[64:128, :], 0.0, [krb[i].b])
        knb = [A.alloc("c_knb%d" % i, (512,), BF16) for i in range(2)]
        vb = [A.alloc("c_vb%d" % i, (4, 128), BF16) for i in range(2)]
        NP = 2 * NQB + 4
        Pt = [A.alloc("c_P%d" % i, (512,), BF16) for i in range(NP)]
        tpair = [A.alloc("c_tp%d" % i, (512,), BF16) for i in range(2)]
        acc = [A.alloc("c_acc%d" % i, (512,), F32) for i in range(NQB)]
        rinv = A.alloc("c_rinv", (512,), F32)
        t1 = A.alloc("c_t1", (512,), F32)
        t2 = A.alloc("c_t2", (512,), F32)
        Ob = banks[0:NQB]
        Sb = banks[4:7]
        b7 = banks[7]
        NKB = S // 512
        L = 2

        def load_lat(kb, r):
            dma("sp", ckb[r].ap, lat_ckv.ap[:, :, kb * 512:(kb + 1) * 512], [lat_ckv.b], [ckb[r].b], "c_ckb%d" % r)
            dma("sp", krb[r].ap[0:64, :], lat_kr.ap[:, kb * 512:(kb + 1) * 512], [lat_kr.b], [krb[r].b], "c_krb%d" % r)

        def gen_k(h, kb):
            r, k2 = kb % NR, kb % 2
            for c in range(2):
                mm(b7.ap, wukv.ap[:, c, h * 256:h * 256 + 128], ckb[r].ap[:, c, :], c == 0, c == 1,
                   [wukv.b, ckb[r].b], [b7.b])
            act(knb[k2].ap, b7.ap, AF.Copy, [b7.b], [knb[k2].b])

        def gen_v(h, kb):
            r, k2 = kb % NR, kb % 2
            b7v = b7.ap.rearrange("p (t d) -> p t d", t=4)
            for t in range(4):
                for c in range(2):
                    mm(b7v[:, t, :], ckb[r].ap[:, c, t * 128:(t + 1) * 128], wukv.ap[:, c, h * 256 + 128:h * 256 + 256],
                       c == 0, c == 1, [wukv.b, ckb[r].b], [b7.b])
            cp(vb[k2].ap, b7v, [b7.b], [vb[k2].b])

        def gen_q(h):
            qi = h % 2
            bl = [Sb[0], Sb[1], Sb[2], b7]
            k = 0
            for qb in range(NQB):
                cs = slice(qb * 512, (qb + 1) * 512)
                bk = bl[k % 4]; k += 1
                for c in range(3):
                    mm(bk.ap, wuq.ap[:, c, h * 192:h * 192 + 128], cqT.ap[:, c, cs], c == 0, c == 2,
                       [wuq.b, cqT.b], [bk.b])
                act(Qn[qi].ap[:, cs], bk.ap, AF.Copy, [bk.b], [Qn[qi].b], scale=QSCALE)
                bk1 = bl[k % 4]; k += 1
                for c in range(3):
                    mm(bk1.ap[0:64, :], wuq.ap[:, c, h * 192 + 128:h * 192 + 192], cqT.ap[:, c, cs], c == 0, c == 2,
                       [wuq.b, cqT.b], [bk1.b])
                bk2 = bl[k % 4]; k += 1
                for c in range(3):
                    mm(bk2.ap[0:64, :], wuq.ap[:, c, 1152 + h * 64:1152 + (h + 1) * 64], cqT.ap[:, c, cs], c == 0, c == 2,
                       [wuq.b, cqT.b], [bk2.b])
                tt(t1.ap[0:64, :], bk1.ap[0:64, :], cosq.ap[0:64, cs], ALU.mult, [bk1.b, cosq.b], [t1.b])
                tt(t2.ap[0:64, :], bk2.ap[0:64, :], sinq.ap[0:64, cs], ALU.mult, [bk2.b, sinq.b], [t2.b])
                tt(Qr[qi].ap[0:64, cs], t1.ap[0:64, :], t2.ap[0:64, :], ALU.add, [t1.b, t2.b], [Qr[qi].b])

        for h in range(NH):
            qi = h % 2
            load_lat(0, 0)
            if NKB > 1:
                load_lat(1, 1)
            gen_q(h)
            gen_k(h, 0)
            gen_v(h, 0)
            items = [(kb, t, qb) for kb in range(NKB) for t in range(4) for qb in range(NQB)]
            n = len(items)
            per_kb = 4 * NQB
            for i in range(n + L):
                if i < n:
                    kb, t, qb = items[i]
                    if i % per_kb == 0:
                        if kb + 2 < NKB:
                            load_lat(kb + 2, (kb + 2) % NR)
                        if kb + 1 < NKB:
                            gen_k(h, kb + 1)
                    if i % per_kb == per_kb // 2 and kb + 1 < NKB:
                        gen_v(h, kb + 1)
                    r, k2 = kb % NR, kb % 2
                    cs = slice(qb * 512, (qb + 1) * 512)
                    sb_ = Sb[i % 3]
                    mm(sb_.ap, knb[k2].ap[:, t * 128:(t + 1) * 128], Qn[qi].ap[:, cs], True, False,
                       [knb[k2].b, Qn[qi].b], [sb_.b])
                    mm(sb_.ap, krb[r].ap[:, t * 128:(t + 1) * 128], Qr[qi].ap[:, cs], False, True,
                       [krb[r].b, Qr[qi].b], [sb_.b])
                    pt = Pt[i % NP]
                    act(pt.ap, sb_.ap, AF.Exp, [sb_.b], [pt.b])
                if i >= L:
                    i2 = i - L
                    kb, t, qb = items[i2]
                    k2 = kb % 2
                    kt = kb * 4 + t
                    pt = Pt[i2 % NP]
                    mm(Ob[qb].ap, vb[k2].ap[:, t, :], pt.ap, kt == 0, kt == NKB * 4 - 1, [vb[k2].b, pt.b], [Ob[qb].b])
                    if kt % 2 == 1:
                        pprev = Pt[(i2 - NQB) % NP]
                        if kt == 1:
                            tt(acc[qb].ap, pprev.ap, pt.ap, ALU.add, [pprev.b, pt.b], [acc[qb].b])
                        else:
                            tpb = tpair[(kt // 2) % 2]
                            tt(tpb.ap, pprev.ap, pt.ap, ALU.add, [pprev.b, pt.b], [tpb.b])
                            tt(acc[qb].ap, acc[qb].ap, tpb.ap, ALU.add, [acc[qb].b, tpb.b], [acc[qb].b])
            for qb in range(NQB):
                cs = slice(qb * 512, (qb + 1) * 512)
                mm(b7.ap, ones_f.ap, acc[qb].ap, True, True, [ones_f.b, acc[qb].b], [b7.b])
                act(rinv.ap, b7.ap, AF.Ln, [b7.b], [rinv.b])
                act(rinv.ap, rinv.ap, AF.Exp, [rinv.b], [rinv.b], scale=-1.0)
                tt(attnT.ap[:, h, cs], Ob[qb].ap, rinv.ap, ALU.mult, [Ob[qb].b, rinv.b], [attnT.b])
        P.barrier()
        A.release(m)

    def phase_d(x_rows, y_rows):
        m = A.mark()
        A.release(CQ_MARK)
        TB = 512
        NB = QJ // TB
        wout = A.alloc("d_wout", (8, D), BF16)
        wd = A.alloc("d_wd", (NFF, D), BF16)
        xt = [A.alloc("d_xt%d" % t, (D,), F32) for t in range(4)]
        mg = [A.alloc("d_mg%d" % i, (D,), F32) for i in range(2)]
        junks = [A.alloc("d_junk%d" % i, (D,), BF16) for i in range(2)]
        junk = junks[0]
        xn = [A.alloc("d_xn%d" % i, (D,), BF16) for i in range(2)]
        sqa = [A.alloc("d_sqa%d" % i, (8, 128), BF16) for i in range(2)]
        scs = [A.alloc("d_sc%d" % i, (4, 4), F32) for i in range(2)]
        sf = A.alloc("d_sf", (4,), F32)
        h2T = [A.alloc("d_h2T%d" % i, (8, TB), BF16) for i in range(2)]
        actT = A.alloc("d_actT", (NFF, TB), BF16)
        sg = [A.alloc("d_sg%d" % i, (TB,), F32) for i in range(2)]
        fst = [A.alloc("d_fst%d" % i, (D,), F32) for i in range(2)]
        x1b = mg
        NRW = 3
        wgu = [A.alloc("d_wgu%d" % i, (2, 8, 128), BF16) for i in range(NRW)]
        dma("sp", wout.ap, s_wout.ap, [s_wout.b], [wout.b], "d_wout")
        tps = [banks[0], banks[7]]
        b7 = banks[7]
        MPb, MAb = banks[1], banks[2]
        b_spill = [Buf("x1spill0"), Buf("x1spill1")]
        state = {"yi": 0, "wd_loaded": False}

        def d1_stages(b):
            st_ = b % 2
            sc = scs[st_]
            h2 = h2T[st_]
            stages = []

            def s1():
                for t in range(4):
                    q0 = b * TB + t * 128
                    qs = slice(q0, q0 + 128)
                    dma("sp", xt[t].ap, x_rows(q0, 128), [], [xt[t].b], "d_xt%d" % t)
                    sq_ = sqa[t % 2]
                    tt(sq_.ap[:, 0:2, :], poolT.ap[:, :, qs], poolT.ap[:, :, qs], ALU.mult, [poolT.b], [sq_.b])
                    tt(sq_.ap[:, 2:8, :], attnT.ap[:, :, qs], attnT.ap[:, :, qs], ALU.mult, [attnT.b], [sq_.b])
                    for c in range(2):
                        mm(b7.ap[:, t:t + 1], sq_.ap[:, c, :], ones_b.ap[:, 0:1], c == 0, c == 1, [sq_.b, ones_b.b], [b7.b])
                    for c in range(6):
                        mm(b7.ap[:, 4 + t:5 + t], sq_.ap[:, 2 + c, :], ones_b.ap[:, 0:1], c == 0, c == 5,
                           [sq_.b, ones_b.b], [b7.b])
                if not state["wd_loaded"]:
                    dma("sp", wd.ap, s_wd.ap, [s_wd.b], [wd.b], "d_wd")
                    state["wd_loaded"] = True
                act(sc.ap[:, 0, :], b7.ap[:, 0:4], AF.Ln, [b7.b, epsc.b], [sc.b], scale=1.0 / 256, bias=epsc.ap)
                act(sc.ap[:, 1, :], b7.ap[:, 4:8], AF.Ln, [b7.b, epsc.b], [sc.b], scale=1.0 / 768, bias=epsc.ap)
                act(sc.ap[:, 0:2, :], sc.ap[:, 0:2, :], AF.Exp, [sc.b], [sc.b], scale=-0.5)
            stages.append(s1)

            def mk_combo(t, half):
                def f():
                    q0 = b * TB + t * 128
                    qs = slice(q0, q0 + 128)
                    hs = slice(half * 512, (half + 1) * 512)
                    g_ = mg[t % 2]
                    for c in range(2):
                        mm(MPb.ap, poolT.ap[:, c, qs], wout.ap[:, c, hs], c == 0, c == 1, [poolT.b, wout.b], [MPb.b])
                    for c in range(6):
                        mm(MAb.ap, attnT.ap[:, c, qs], wout.ap[:, 2 + c, hs], c == 0, c == 5, [attnT.b, wout.b], [MAb.b])
                    ts(g_.ap[:, hs], MPb.ap, sc.ap[:, 0, t:t + 1], None, ALU.mult, None, [MPb.b, sc.b], [g_.b])
                    stt(g_.ap[:, hs], MAb.ap, sc.ap[:, 1, t:t + 1], g_.ap[:, hs], ALU.mult, ALU.add,
                        [MAb.b, sc.b, g_.b], [g_.b])
                return f

            def mk_s3(t0):
                def f():
                    for t in (t0, t0 + 1):
                        act(junks[t % 2].ap, mg[t % 2].ap, AF.Square, [mg[t % 2].b], [junks[t % 2].b, sc.b],
                            scale=float(D) ** -0.5, accum_out=sc.ap[:, 2, t:t + 1])
                    act(sc.ap[:, 2, t0:t0 + 2], sc.ap[:, 2, t0:t0 + 2], AF.Ln, [sc.b, epsc.b], [sc.b], bias=epsc.ap)
                    act(sc.ap[:, 2, t0:t0 + 2], sc.ap[:, 2, t0:t0 + 2], AF.Exp, [sc.b], [sc.b], scale=-0.5)
                return f

            def mk_s4(t0):
                def f():
                    for t in (t0, t0 + 1):
                        g_ = mg[t % 2]
                        q0 = b * TB + t * 128
                        stt(g_.ap, g_.ap, sc.ap[:, 2, t:t + 1], G1b.ap, ALU.mult, ALU.mult, [g_.b, sc.b, G1b.b], [g_.b])
                        tt(xt[t].ap, g_.ap, xt[t].ap, ALU.add, [g_.b, xt[t].b], [xt[t].b])
                        dma("pool", y_rows(q0, 128), xt[t].ap, [xt[t].b], [b_spill[st_]], "d_sp%d" % t)
                return f

            def mk_s5(t0):
                def f():
                    for t in (t0, t0 + 1):
                        act(junks[t % 2].ap, xt[t].ap, AF.Square, [xt[t].b], [junks[t % 2].b, sc.b],
                            scale=float(D) ** -0.5, accum_out=sc.ap[:, 3, t:t + 1])
                    act(sc.ap[:, 3, t0:t0 + 2], sc.ap[:, 3, t0:t0 + 2], AF.Ln, [sc.b, epsc.b], [sc.b], bias=epsc.ap)
                    act(sc.ap[:, 3, t0:t0 + 2], sc.ap[:, 3, t0:t0 + 2], AF.Exp, [sc.b], [sc.b], scale=-0.5)
                    for t in (t0, t0 + 1):
                        act(xn[t % 2].ap, xt[t].ap, AF.Copy, [xt[t].b, sc.b], [xn[t % 2].b], scale=sc.ap[:, 3, t:t + 1])
                return f

            def mk_tr(t):
                def f():
                    tp = tps[t % 2]
                    tpv = tp.ap.bitcast(BF16).rearrange("p (c n) -> p c n", c=8)
                    for c in range(8):
                        tr(tpv[:, c, :], xn[t % 2].ap[:, c * 128:(c + 1) * 128], idb.ap, [xn[t % 2].b, idb.b], [tp.b])
                    for c in range(8):
                        ts(h2.ap[:, c, t * 128:(t + 1) * 128], tpv[:, c, :], A2c.ap[:, c:c + 1], B2c[:, c:c + 1],
                           ALU.mult, ALU.add, [tp.b, A2c.b, modc.b], [h2.b])
                return f

            for t0 in (0, 2):
                for t in (t0, t0 + 1):
                    for half in range(2):
                        stages.append(mk_combo(t, half))
                stages.append(mk_s3(t0))
                stages.append(mk_s4(t0))
                stages.append(mk_s5(t0))
                stages.append(None)
                stages.append(mk_tr(t0))
                stages.append(mk_tr(t0 + 1))
            return stages

        def d2_unit(b, j, u):
            h2 = h2T[b % 2]
            r = u % NRW
            dma("sp", wgu[r].ap, s_wgu.ap[j], [s_wgu.b], [wgu[r].b], "d_wgu%d" % r)
            gb, ub = banks[3 + 2 * (u % 2)], banks[4 + 2 * (u % 2)]
            for c in range(8):
                mm(gb.ap, wgu[r].ap[:, 0, c, :], h2.ap[:, c, :], c == 0, c == 7, [wgu[r].b, h2.b], [gb.b])
            for c in range(8):
                mm(ub.ap, wgu[r].ap[:, 1, c, :], h2.ap[:, c, :], c == 0, c == 7, [wgu[r].b, h2.b], [ub.b])
            s_ = sg[u % 2]
            act(s_.ap, gb.ap, AF.Exp, [gb.b], [s_.b], scale=-1.0)
            act(s_.ap, s_.ap, AF.Ln, [s_.b, ones_f.b], [s_.b], bias=ones_f.ap[:, 0:1])
            act(s_.ap, s_.ap, AF.Exp, [s_.b], [s_.b], scale=-1.0)
            tt(s_.ap, s_.ap, gb.ap, ALU.mult, [s_.b, gb.b], [s_.b])
            tt(actT.ap[:, j, :], s_.ap, ub.ap, ALU.mult, [s_.b, ub.b], [actT.b])

        def d3(b):
            for t in range(4):
                q0 = b * TB + t * 128
                yi = state["yi"]
                f_ = fst[yi % 2]
                x1_ = x1b[yi % 2]
                dma("sp", x1_.ap, y_rows(q0, 128), [b_spill[b % 2]], [x1_.b], "d_x1b%d" % (yi % 2))
                for half in range(2):
                    hs = slice(half * 512, (half + 1) * 512)
                    fb = banks[1 + half]
                    for j in range(NFF):
                        mm(fb.ap, actT.ap[:, j, t * 128:(t + 1) * 128], wd.ap[:, j, hs], j == 0, j == NFF - 1,
                           [actT.b, wd.b], [fb.b])
                    act(f_.ap[:, hs], fb.ap, AF.Copy, [fb.b], [f_.b])
                k = yi % 4
                act(junk.ap, f_.ap, AF.Square, [f_.b], [junk.b, sf.b], scale=float(D) ** -0.5, accum_out=sf.ap[:, k:k + 1])
                act(sf.ap[:, k:k + 1], sf.ap[:, k:k + 1], AF.Ln, [sf.b, epsc.b], [sf.b], bias=epsc.ap)
                act(sf.ap[:, k:k + 1], sf.ap[:, k:k + 1], AF.Exp, [sf.b], [sf.b], scale=-0.5)
                stt(f_.ap, f_.ap, sf.ap[:, k:k + 1], G2b.ap, ALU.mult, ALU.mult, [f_.b, sf.b, G2b.b], [f_.b])
                tt(f_.ap, f_.ap, x1_.ap, ALU.add, [f_.b, x1_.b], [f_.b])
                dma("pool", y_rows(q0, 128), f_.ap, [f_.b, x1_.b], [b_yout], "d_y%d" % (yi % 2), is_out=True)
                state["yi"] = yi + 1

        for f in d1_stages(0):
            if f:
                f()
        u = 0
        for b in range(NB):
            stages = d1_stages(b + 1) if b + 1 < NB else []
            for j in range(NFF):
                d2_unit(b, j, u)
                u += 1
                if stages:
                    f = stages.pop(0)
                    if f:
                        f()
            while stages:
                f = stages.pop(0)
                if f:
                    f()
            d3(b)
        P.barrier()
        A.release(m)

    m0 = A.mark()
    dma("sp", gcols.ap, gcols_d, [], [gcols.b], "gcols")
    dma("sp", badac.ap, badac_d, [], [badac.b], "badac")
    mod_both()
    init()
    P.barrier()
    A.release(m0)
    phase_a(lambda r0, n: xs_d[r0:r0 + n, :], SS)
    init_b()
    phase_b(lambda r0, n: xsq_d[8 + r0:8 + r0 + n, :], (xsq_d[0:8, :], xsq_d[8 + QJ:16 + QJ, :]), 0)
    phase_c(SS, cosq_d, sinq_d)
    phase_d(lambda r0, n: xsq_d[8 + r0:8 + r0 + n, :], lambda r0, n: ys_d[r0:r0 + n, :])
    load_mod1()
    phase_a(lambda r0, n: xp_d[8 + r0:8 + r0 + n, :], SP)
    for j in range(NJP):
        j0 = j * QJ
        phase_b(lambda r0, n, j0=j0: xp_d[8 + j0 + r0:8 + j0 + r0 + n, :],
                (xp_d[j0:j0 + 8, :], xp_d[8 + j0 + QJ:16 + j0 + QJ, :]), 1 + j)
        phase_c(SP, cosk_d[:, j0:j0 + QJ], sink_d[:, j0:j0 + QJ])
        phase_d(lambda r0, n, j0=j0: xp_d[8 + j0 + r0:8 + j0 + r0 + n, :],
                lambda r0, n, j0=j0: yp_d[j0 + r0:j0 + r0 + n, :])

    P.finalize(nc, st)
    with nc.Block() as block:
        @block.tensor
        def _(e):
            P.emit("pe", e)

        @block.scalar
        def _(e):
            P.emit("act", e)

        @block.vector
        def _(e):
            P.emit("dve", e)

        @block.gpsimd
        def _(e):
            P.emit("pool", e)

        @block.sync
        def _(e):
            P.emit("sp", e)
            P.final_waits(e)
    st.close()
    build.stats = {k: len(v) for k, v in P.q.items()}
    build.peak = A.peak
    return nc


def rope_tables(S):
    inv_freq = (np.float32(10000.0) ** (-(np.arange(0, 64, 2, dtype=np.float32)) / np.float32(64))).astype(np.float32)
    ang = (np.arange(S, dtype=np.float32)[:, None] * inv_freq[None, :]).astype(np.float32)
    cos = np.cos(ang).astype(np.float32).T
    sin = np.sin(ang).astype(np.float32).T
    cos2 = np.concatenate([cos, cos], 0)
    sins = np.concatenate([-sin, sin], 0)
    return np.ascontiguousarray(cos2), np.ascontiguousarray(sins)


def pool_tables(S, start, n):
    t = np.arange(start, start + n)
    invc = np.zeros((128, 2, n), np.float32)
    for g, w in enumerate((2, 4, 8, 16)):
        lo = np.clip(t - w // 2, 0, S)
        hi = np.clip(t + w - w // 2, 0, S)
        v = (1.0 / (hi - lo)).astype(np.float32)
        c, half = g // 2, g % 2
        invc[half * 64:(half + 1) * 64, c, :] = v[None, :]
    hp = np.concatenate([np.arange(start - 8, start), np.arange(start + n, start + n + 8)])
    hm = ((hp >= 0) & (hp < S)).astype(np.float32)
    return invc, np.ascontiguousarray(np.broadcast_to(hm[None, :], (128, 16)))


def col_layout(v):
    return np.ascontiguousarray(v.reshape(-1, 128).T)


_NC_CACHE = {}


def make_in_maps(cfg, inp):
    SP, SS, QJ, NJP, SMAX = cfg.SP, cfg.SS, cfg.QJ, cfg.NJP, cfg.SMAX
    f = lambda a: np.ascontiguousarray(np.asarray(a, dtype=np.float32))
    w_in = f(inp["w_in"][0])
    w_in_x = np.concatenate([w_in, w_in[:, 928:960], w_in[:, 896:928]], axis=1)
    w_uq = f(inp["w_uq"][0])
    sw = []
    for h in range(NH):
        b = h * 192 + 128
        sw += [w_uq[:, b + 32:b + 64], w_uq[:, b:b + 32]]
    w_uq_x = np.concatenate([w_uq] + sw, axis=1)
    pool_w = f(inp["pool_w"][0])
    pw_bd = np.zeros((2, 128, 128), np.float32)
    for g in range(4):
        c, hf = g // 2, g % 2
        pw_bd[c, hf * 64:(hf + 1) * 64, hf * 64:(hf + 1) * 64] = pool_w[g]
    gcols = np.concatenate([
        col_layout(f(inp["g_mix_pre"][0])), col_layout(f(inp["g_ffn_pre"][0])), col_layout(f(inp["g_q_a"][0])),
        col_layout(f(inp["g_kv_a"][0])), col_layout(f(inp["pool_scale"][0])),
        col_layout(np.concatenate([f(inp["g_pool_out"][0]), f(inp["g_attn_out"][0])]))], axis=1)
    gpost = np.stack([f(inp["g_mix_post"][0]), f(inp["g_ffn_post"][0])])
    b_ada = f(inp["b_ada"][0])
    cosk, sink = rope_tables(SMAX)
    xs = f(inp["x_sample"][0])
    xs_pad = np.concatenate([np.zeros((8, D), np.float32), xs, np.zeros((8, D), np.float32)], 0)
    shared = {
        "xs": xs, "w_ada": f(inp["w_ada"][0]), "b_ada_col": col_layout(b_ada), "b_ada_row": b_ada,
        "gcols": np.ascontiguousarray(gcols), "gpost": np.ascontiguousarray(gpost), "w_in_x": np.ascontiguousarray(w_in_x),
        "w_uq_x": np.ascontiguousarray(w_uq_x), "w_ukv": f(inp["w_ukv"][0]), "w_out": f(inp["w_out"][0]),
        "w_gate": f(inp["w_gate"][0]), "w_up": f(inp["w_up"][0]), "w_down": f(inp["w_down"][0]), "pw_bd": pw_bd,
        "cosk": cosk, "sink": sink, "ident": np.eye(128, dtype=np.float32),
    }
    ptabs = [pool_tables(SP, j * QJ, QJ) for j in range(NJP)]
    maps = []
    for c in range(NCORES):
        xp = f(inp["x_prompt"][c])
        st_ = pool_tables(SS, c * QJ, QJ)
        m = dict(shared)
        m["xp"] = np.concatenate([np.zeros((8, D), np.float32), xp, np.zeros((8, D), np.float32)], 0)
        m["xsq"] = np.ascontiguousarray(xs_pad[c * QJ:c * QJ + QJ + 16])
        m["cvec"] = np.stack([col_layout(f(inp["c_sample"][0])), col_layout(f(inp["c_prompt"][c]))])
        m["cosq"] = np.ascontiguousarray(cosk[:, c * QJ:(c + 1) * QJ])
        m["sinq"] = np.ascontiguousarray(sink[:, c * QJ:(c + 1) * QJ])
        m["invc"] = np.stack([st_[0]] + [p[0] for p in ptabs])
        m["hmask"] = np.stack([st_[1]] + [p[1] for p in ptabs])
        maps.append(m)
    return maps


def run(cfg, inp, trace=False):
    key = (cfg.SP, cfg.SS, cfg.NQB)
    if key not in _NC_CACHE:
        _NC_CACHE[key] = build(cfg)
    nc = _NC_CACHE[key]
    maps = make_in_maps(cfg, inp)
    res = run_bass_kernel_spmd(nc, maps, core_ids=list(range(NCORES)), trace=trace)
    yp = np.stack([res.results[c]["yp"] for c in range(NCORES)], 0)
    ys = np.concatenate([res.results[c]["ys"] for c in range(NCORES)], 0)[None]
    return (yp.astype(np.float32), ys.astype(np.float32)), res


def kernel(**inputs):
    cfg = Cfg()
    out, _ = run(cfg, inputs)
    return out
```

```python
import os
import numpy as np
from contextlib import ExitStack
import concourse.bass as bass
import concourse.mybir as mybir
from concourse.bass_utils import run_bass_kernel_spmd

F32 = mybir.dt.float32
BF16 = mybir.dt.bfloat16
AF = mybir.ActivationFunctionType
ALU = mybir.AluOpType

D = 1024
DFF = 2816
NFF = 22
NH = 6
EPS = 1e-6
QSCALE = 192.0 ** -0.5
NCORES = 8
ENGS = ("pe", "act", "dve", "pool", "sp")
STRICT_SAME_ENGINE = os.environ.get("KSTRICT") == "1"


class Buf:
    __slots__ = ("name", "writers", "readers", "war")

    def __init__(self, name):
        self.name = name
        self.writers = []
        self.readers = []
        self.war = []


class T:
    __slots__ = ("ap", "b")

    def __init__(self, ap, name):
        self.ap = ap
        self.b = Buf(name)


class Op:
    __slots__ = ("eng", "fn", "deps", "needs_inc", "ticket", "dma", "semkey", "cpos", "barrier")

    def __init__(self, eng, fn, dma, semkey):
        self.eng = eng
        self.fn = fn
        self.deps = []
        self.needs_inc = False
        self.ticket = 0
        self.dma = dma
        self.semkey = semkey
        self.cpos = 0
        self.barrier = False


class Prog:
    def __init__(self):
        self.q = {e: [] for e in ENGS}
        self.ncomp = {e: 0 for e in ENGS}
        self.dma_count = {}
        self.last_dma = {}
        self.last_comp = {}
        self.out_ops = []
        self.nobarrier = set(("s_wd", "s_wg", "s_wu"))

    def op(self, eng, fn, reads=(), writes=(), dma=False, semkey=None, is_out=False):
        o = Op(eng, fn, dma, semkey)
        deps = set()
        for b in reads:
            for w in b.writers:
                deps.add(w)
        for b in writes:
            if b.readers:
                for r in b.readers:
                    deps.add(r)
                for w in b.writers:
                    deps.add(w)
            else:
                for r in b.war:
                    deps.add(r)
        deps.discard(o)
        o.deps = list(deps)
        for b in writes:
            if b.readers:
                b.war = b.readers
                b.readers = []
                b.writers = [o]
            else:
                b.writers.append(o)
        for b in reads:
            if b not in writes:
                b.readers.append(o)
        if not dma:
            self.ncomp[eng] += 1
            self.last_comp[eng] = o
        o.cpos = self.ncomp[eng]
        self.q[eng].append(o)
        if dma:
            c = self.dma_count.get(semkey, 0) + 16
            self.dma_count[semkey] = c
            o.ticket = c
            o.needs_inc = True
            self.last_dma[semkey] = o
        if is_out:
            self.out_ops.append(o)
        return o

    def barrier(self):
        front = list(self.last_comp.values()) + [o for k, o in self.last_dma.items() if k not in self.nobarrier]
        for e in ENGS:
            o = Op(e, None, False, None)
            o.barrier = True
            o.deps = [x for x in front if x.dma or x.eng != e]
            o.cpos = self.ncomp[e]
            self.q[e].append(o)

    @staticmethod
    def _need_sem(x, y):
        if x.dma:
            return True
        if x.eng == y.eng:
            if x.eng == "pe":
                return False
            if y.dma:
                return True
            if x.eng in ("act", "dve"):
                return STRICT_SAME_ENGINE or (y.cpos - x.cpos) < 3
            return True
        return True

    def finalize(self, nc, stack):
        for e in ENGS:
            for y in self.q[e]:
                y.deps = [x for x in y.deps if self._need_sem(x, y)]
                for x in y.deps:
                    if not x.dma:
                        x.needs_inc = True
        for e in ENGS:
            t = 0
            for o in self.q[e]:
                if not o.dma and o.needs_inc:
                    t += 1
                    o.ticket = t
        self.esem = {e: stack.enter_context(nc.semaphore("s_" + e)) for e in ENGS}
        self.dsem = {k: stack.enter_context(nc.semaphore("d_%d" % i)) for i, k in enumerate(self.dma_count)}

    def emit(self, eng_name, eng):
        waited = {}
        for o in self.q[eng_name]:
            need = {}
            for x in o.deps:
                s = self.dsem[x.semkey] if x.dma else self.esem[x.eng]
                k = id(s)
                if k not in need or need[k][1] < x.ticket:
                    need[k] = (s, x.ticket)
            for k, (s, v) in need.items():
                if waited.get(k, 0) >= v:
                    continue
                eng.wait_ge(s, v)
                waited[k] = v
            if o.fn is None:
                continue
            ins = o.fn(eng)
            if o.needs_inc:
                if o.dma:
                    ins.then_inc(self.dsem[o.semkey], 16)
                else:
                    ins.then_inc(self.esem[o.eng], 1)

    def final_waits(self, eng):
        done = {}
        for o in self.out_ops:
            done[o.semkey] = max(done.get(o.semkey, 0), o.ticket)
        for k, v in done.items():
            eng.wait_ge(self.dsem[k], v)


class Arena:
    def __init__(self, tensor, nbytes):
        self.t = tensor
        self.cap = nbytes
        self.off = 0
        self.peak = 0

    def alloc(self, name, shape, dtype):
        esz = 4 if dtype == F32 else 2
        n = 1
        for s in shape:
            n *= s
        nb = n * esz
        off = (self.off + 63) // 64 * 64
        assert off + nb <= self.cap, "arena overflow at %s: %d + %d > %d" % (name, off, nb, self.cap)
        self.off = off + nb
        self.peak = max(self.peak, self.off)
        v = self.t[:, off // 2:(off + nb) // 2]
        if dtype == F32:
            v = v.bitcast(F32)
        if len(shape) == 2:
            v = v.rearrange("p (a b) -> p a b", a=shape[0])
        elif len(shape) == 3:
            v = v.rearrange("p (a b c) -> p a b c", a=shape[0], b=shape[1])
        return T(v, name)

    def mark(self):
        return self.off

    def release(self, m):
        self.off = m


class Cfg:
    def __init__(self, SP=8192, SS=16384, NQB=4):
        self.SP, self.SS, self.NQB = SP, SS, NQB
        self.QJ = 512 * NQB
        self.NJP = SP // self.QJ
        assert SS // NCORES == self.QJ and SP % self.QJ == 0
        self.SMAX = max(SP, SS)


def build(cfg):
    SP, SS, NQB, QJ, NJP, SMAX = cfg.SP, cfg.SS, cfg.NQB, cfg.QJ, cfg.NJP, cfg.SMAX
    nc = bass.Bass("TRN2", target_bir_lowering=False)

    def din(name, shape):
        return nc.dram_tensor(name, list(shape), F32, kind="ExternalInput").ap()

    xp_d = din("xp", (SP + 16, D))
    xs_d = din("xs", (SS, D))
    xsq_d = din("xsq", (QJ + 16, D))
    cvec_d = din("cvec", (2, 128, 8))
    wada_d = din("w_ada", (D, 6 * D))
    badac_d = din("b_ada_col", (128, 48))
    badar_d = din("b_ada_row", (6 * D,))
    gcols_d = din("gcols", (128, 31))
    gpost_d = din("gpost", (2, D))
    win_d = din("w_in_x", (D, 1024))
    wuq_d = din("w_uq_x", (384, 1536))
    wukv_d = din("w_ukv", (256, 1536))
    wout_d = din("w_out", (D, D))
    wg_d = din("w_gate", (D, DFF))
    wu_d = din("w_up", (D, DFF))
    wd_d = din("w_down", (DFF, D))
    pwbd_d = din("pw_bd", (2, 128, 128))
    cosk_d = din("cosk", (64, SMAX))
    sink_d = din("sink", (64, SMAX))
    cosq_d = din("cosq", (64, QJ))
    sinq_d = din("sinq", (64, QJ))
    invc_d = din("invc", (1 + NJP, 128, 2, QJ))
    hmask_d = din("hmask", (1 + NJP, 128, 16))
    ident_d = din("ident", (128, 128))
    yp_d = nc.dram_tensor("yp", [SP, D], F32, kind="ExternalOutput").ap()
    ys_d = nc.dram_tensor("ys", [QJ, D], F32, kind="ExternalOutput").ap()

    def dscr(name, shape):
        return T(nc.dram_tensor(name, list(shape), BF16).ap(), name)

    s_win = dscr("s_win", (128, 8, 1024))
    s_wuq = dscr("s_wuq", (128, 3, 1536))
    s_wukv = dscr("s_wukv", (128, 2, 1536))
    s_wout = dscr("s_wout", (128, 8, 1024))
    s_wd = dscr("s_wd", (128, NFF, 1024))
    s_wgu = dscr("s_wgu", (NFF, 128, 2, 8, 128))
    lat_ckv = dscr("lat_ckv", (128, 2, SMAX))
    lat_kr = dscr("lat_kr", (64, SMAX))
    b_yout = Buf("yout")

    P = Prog()
    st = ExitStack()
    ARENA_BYTES = 207 * 1024
    arena_t = st.enter_context(nc.sbuf_tensor("arena", [128, ARENA_BYTES // 2], BF16))
    A = Arena(arena_t, ARENA_BYTES)
    banks = [T(st.enter_context(nc.psum_tensor("bk%d" % i, [128, 512], F32))[:], "bk%d" % i) for i in range(8)]

    def mm(out, lhsT, rhs, start, stop, R, W):
        P.op("pe", lambda e: e.matmul(out, lhsT=lhsT, rhs=rhs, start=start, stop=stop), reads=R, writes=W)

    def tr(out, in_, ident, R, W):
        P.op("pe", lambda e: e.transpose(out=out, in_=in_, identity=ident), reads=R, writes=W)

    def act(out, in_, func, R, W, **kw):
        P.op("act", lambda e: e.activation(out=out, in_=in_, func=func, **kw), reads=R, writes=W)

    def tt(out, in0, in1, op, R, W, eng="dve"):
        P.op(eng, lambda e: e.tensor_tensor(out=out, in0=in0, in1=in1, op=op), reads=R, writes=W)

    def ts(out, in0, s1, s2, op0, op1, R, W, eng="dve"):
        if op1 is None:
            P.op(eng, lambda e: e.tensor_scalar(out=out, in0=in0, scalar1=s1, scalar2=None, op0=op0),
                 reads=R, writes=W)
        else:
            P.op(eng, lambda e: e.tensor_scalar(out=out, in0=in0, scalar1=s1, scalar2=s2, op0=op0, op1=op1),
                 reads=R, writes=W)

    def stt(out, in0, scalar, in1, op0, op1, R, W, eng="dve"):
        P.op(eng, lambda e: e.scalar_tensor_tensor(out=out, in0=in0, scalar=scalar, in1=in1, op0=op0, op1=op1),
             reads=R, writes=W)

    def cp(out, in_, R, W, eng="dve"):
        P.op(eng, lambda e: e.tensor_copy(out=out, in_=in_), reads=R, writes=W)

    def memset(ap, val, W, eng="dve"):
        P.op(eng, lambda e: e.memset(ap, val), writes=W)

    def dma(eng, out, in_, R, W, key, is_out=False):
        P.op(eng, lambda e: e.dma_start(out=out, in_=in_), reads=R, writes=W, dma=True, semkey=key, is_out=is_out)

    idb = A.alloc("idb", (128,), BF16)
    ones_b = A.alloc("ones_b", (128,), BF16)
    ones_f = A.alloc("ones_f", (128,), F32)
    epsc = A.alloc("epsc", (1,), F32)
    gcols = A.alloc("gcols", (31,), F32)
    badac = A.alloc("badac", (48,), F32)
    modc = A.alloc("modc", (32,), F32)
    A1c = A.alloc("A1c", (8,), F32)
    A2c = A.alloc("A2c", (8,), F32)
    G1b = A.alloc("G1b", (D,), F32)
    G2b = A.alloc("G2b", (D,), F32)
    pwbd = A.alloc("pwbd", (2, 128), BF16)
    poolT = A.alloc("poolT", (2, QJ), BF16)
    attnT = A.alloc("attnT", (NH, QJ), BF16)
    G_Q, G_KV, G_PS, G_OUT = 16, 19, 21, 23
    CQ_MARK = A.mark()
    cqT = A.alloc("cqT", (3, QJ), BF16)
    PERSIST = A.mark()

    def init():
        idf = A.alloc("idf", (128,), F32)
        dma("sp", idf.ap, ident_d, [], [idf.b], "idf")
        cp(idb.ap, idf.ap, [idf.b], [idb.b])
        memset(ones_b.ap, 1.0, [ones_b.b])
        memset(ones_f.ap, 1.0, [ones_f.b])
        memset(epsc.ap, EPS, [epsc.b])
        dma("pool", pwbd.ap, pwbd_d.rearrange("c p m -> p c m"), [], [pwbd.b], "pwbd")
        dma("pool", s_win.ap, win_d.rearrange("(c p) m -> p c m", p=128), [], [s_win.b], "s_win")
        dma("pool", s_wuq.ap, wuq_d.rearrange("(c p) m -> p c m", p=128), [], [s_wuq.b], "s_wuq")
        dma("pool", s_wukv.ap, wukv_d.rearrange("(c p) m -> p c m", p=128), [], [s_wukv.b], "s_wukv")
        stf = [A.alloc("stf%d" % i, (D,), F32) for i in range(2)]
        stb = [A.alloc("stb%d" % i, (D,), BF16) for i in range(2)]
        for c in range(8):
            i = c % 2
            dma("sp", stf[i].ap, wout_d[c * 128:(c + 1) * 128, :], [], [stf[i].b], "stf%d" % i)
            ts(stb[i].ap, stf[i].ap, gcols.ap[:, G_OUT + c:G_OUT + c + 1], None, ALU.mult, None,
               [stf[i].b, gcols.b], [stb[i].b])
            dma("sp", s_wout.ap[:, c, :], stb[i].ap, [stb[i].b], [s_wout.b], "stb%d" % i)

    def init_b():
        dma("pool", s_wd.ap, wd_d.rearrange("(j p) n -> p j n", p=128), [], [s_wd.b], "s_wd")
        for j in range(NFF):
            dma("pool", s_wgu.ap[j, :, 0], wg_d[:, j * 128:(j + 1) * 128].rearrange("(c p) m -> p c m", p=128),
                [], [s_wgu.b], "s_wg")
            dma("pool", s_wgu.ap[j, :, 1], wu_d[:, j * 128:(j + 1) * 128].rearrange("(c p) m -> p c m", p=128),
                [], [s_wgu.b], "s_wu")

    s_modc = T(nc.dram_tensor("s_modc", [128, 48], F32).ap(), "s_modc")
    s_G = T(nc.dram_tensor("s_G", [128, 2, D], F32).ap(), "s_G")

    def mod_both():
        wad = [A.alloc("wad%d" % i, (8, 512), BF16) for i in range(12)]
        order = [(mi, half) for mi in (0, 1, 3, 4) for half in range(2)] + [(mi, half) for mi in (2, 5) for half in range(2)]
        for nb, (mi, half) in enumerate(order):
            col0 = mi * D + half * 512
            dma("pool", wad[nb].ap, wada_d[:, col0:col0 + 512].rearrange("(c p) m -> p c m", p=128),
                [], [wad[nb].b], "wad%d" % nb)
        rowb = [A.alloc("rowb%d" % i, (512,), F32) for i in range(4)]
        rowg = [A.alloc("rowg%d" % i, (512,), F32) for i in range(4)]
        for k, (mi, half) in enumerate(order[8:]):
            col0 = mi * D + half * 512
            dma("sp", rowb[k].ap, badar_d[col0:col0 + 512].partition_broadcast(128), [], [rowb[k].b], "rowb%d" % k)
            dma("sp", rowg[k].ap, gpost_d[k // 2, half * 512:(half + 1) * 512].partition_broadcast(128),
                [], [rowg[k].b], "rowg%d" % k)
        tmpc = A.alloc("m_tmpc", (48,), F32)
        tmpG = A.alloc("m_tmpG", (2, D), F32)
        rsum = A.alloc("m_rsum", (512,), F32)
        for si in range(2):
            cc = A.alloc("cc%d" % si, (8,), F32)
            csb = A.alloc("csb%d" % si, (8,), BF16)
            lhsb = A.alloc("lhsb%d" % si, (8, 128), BF16)
            dma("sp", cc.ap, cvec_d[si], [], [cc.b], "cc%d" % si)
            act(csb.ap, cc.ap, AF.Silu, [cc.b], [csb.b])
            for c in range(8):
                cp(lhsb.ap[:, c, :], csb.ap[:, c:c + 1].to_broadcast([128, 128]), [csb.b], [lhsb.b])
            pcol = banks[si]
            colmods = (0, 1, 3, 4)
            for nb in range(8):
                ci, half = nb // 2, nb % 2
                w = wad[nb]
                for m4 in range(4):
                    idx = ci * 8 + half * 4 + m4
                    for c in range(8):
                        mm(pcol.ap[:, idx:idx + 1], w.ap[:, c, m4 * 128:(m4 + 1) * 128], csb.ap[:, c:c + 1],
                           c == 0, c == 7, [w.b, csb.b], [pcol.b])
            if si == 0:
                mc, a1, a2, mcb, a1b, a2b = modc.ap, A1c.ap, A2c.ap, modc.b, A1c.b, A2c.b
            else:
                mc, a1, a2 = tmpc.ap[:, 0:32], tmpc.ap[:, 32:40], tmpc.ap[:, 40:48]
                mcb = a1b = a2b = tmpc.b
            for ci, mi in enumerate(colmods):
                tt(mc[:, ci * 8:(ci + 1) * 8], pcol.ap[:, ci * 8:(ci + 1) * 8], badac.ap[:, mi * 8:(mi + 1) * 8],
                   ALU.add, [pcol.b, badac.b], [mcb])
            stt(a1, mc[:, 8:16], 1.0, gcols.ap[:, 0:8], ALU.add, ALU.mult, [mcb, gcols.b], [a1b])
            stt(a2, mc[:, 24:32], 1.0, gcols.ap[:, 8:16], ALU.add, ALU.mult, [mcb, gcols.b], [a2b])
            for k in range(4):
                gi, half = k // 2, k % 2
                w = wad[8 + k]
                pr = banks[2 + (k % 2) + 2 * si]
                for c in range(8):
                    mm(pr.ap, lhsb.ap[:, c, :], w.ap[:, c, :], c == 0, c == 7, [lhsb.b, w.b], [pr.b])
                tt(rsum.ap, pr.ap, rowb[k].ap, ALU.add, [pr.b, rowb[k].b], [rsum.b])
                if si == 0:
                    G = (G1b, G2b)[gi]
                    tt(G.ap[:, half * 512:(half + 1) * 512], rsum.ap, rowg[k].ap, ALU.mult, [rsum.b, rowg[k].b], [G.b])
                else:
                    tt(tmpG.ap[:, gi, half * 512:(half + 1) * 512], rsum.ap, rowg[k].ap, ALU.mult,
                       [rsum.b, rowg[k].b], [tmpG.b])
            if si == 1:
                dma("sp", s_modc.ap, tmpc.ap, [tmpc.b], [s_modc.b], "m_oc")
                dma("sp", s_G.ap, tmpG.ap, [tmpG.b], [s_G.b], "m_oG")

    def load_mod1():
        dma("sp", modc.ap, s_modc.ap[:, 0:32], [s_modc.b], [modc.b], "l_modc")
        dma("sp", A1c.ap, s_modc.ap[:, 32:40], [s_modc.b], [A1c.b], "l_a1")
        dma("sp", A2c.ap, s_modc.ap[:, 40:48], [s_modc.b], [A2c.b], "l_a2")
        dma("sp", G1b.ap, s_G.ap[:, 0, :], [s_G.b], [G1b.b], "l_g1")
        dma("sp", G2b.ap, s_G.ap[:, 1, :], [s_G.b], [G2b.b], "l_g2")

    B1c = modc.ap[:, 0:8]
    B2c = modc.ap[:, 16:24]

    class Front:
        def __init__(self, tag, nx=8, nxn=8):
            self.xt = [A.alloc("%s_xt%d" % (tag, i), (D,), F32) for i in range(nx)]
            self.junk = [A.alloc("%s_junk%d" % (tag, i), (D,), BF16) for i in range(4)]
            self.xn = [A.alloc("%s_xn%d" % (tag, i), (D,), BF16) for i in range(nxn)]
            self.ss = [A.alloc("%s_ss%d" % (tag, i), (4,), F32) for i in range(2)]
            self.i = 0
            self.g = 0
            self.tag = tag

        def part1(self, tiles, n):
            k = len(tiles)
            ss = self.ss[self.g % 2]
            xns = [self.xn[(self.g % 2) * 4 + t] for t in range(k)] if len(self.xn) >= 8 else [self.xn[t] for t in range(k)]
            self.g += 1
            xts = []
            for srcs in tiles:
                xi = self.i % len(self.xt)
                self.i += 1
                xt = self.xt[xi]
                xts.append(xt)
                for (ap, r0, nr) in srcs:
                    dma("sp", xt.ap[r0:r0 + nr, :], ap, [], [xt.b], "%s_xt%d" % (self.tag, xi))
            for t, xt in enumerate(xts):
                act(self.junk[t].ap[0:n, :], xt.ap[0:n, :], AF.Square, [xt.b], [self.junk[t].b, ss.b],
                    scale=float(D) ** -0.5, accum_out=ss.ap[0:n, t:t + 1])
            act(ss.ap[0:n, 0:k], ss.ap[0:n, 0:k], AF.Ln, [ss.b, epsc.b], [ss.b], bias=epsc.ap[0:n, :])
            act(ss.ap[0:n, 0:k], ss.ap[0:n, 0:k], AF.Exp, [ss.b], [ss.b], scale=-0.5)
            for t, xt in enumerate(xts):
                act(xns[t].ap[0:n, :], xt.ap[0:n, :], AF.Copy, [xt.b, ss.b], [xns[t].b], scale=ss.ap[0:n, t:t + 1])
            return xns

        def part2_tile(self, xns, t, n, Acol, Bcol, ABb, hT, tps):
            xn = xns[t]
            tp = tps[t % len(tps)]
            tpv = tp.ap.bitcast(BF16).rearrange("p (c n) -> p c n", c=8)
            for c in range(8):
                tr(tpv[:, c, 0:n], xn.ap[0:n, c * 128:(c + 1) * 128], idb.ap[0:n, 0:n], [xn.b, idb.b], [tp.b])
            for c in range(8):
                ts(hT.ap[:, c, t * n:(t + 1) * n], tpv[:, c, 0:n], Acol[:, c:c + 1], Bcol[:, c:c + 1],
                   ALU.mult, ALU.add, [tp.b] + ABb, [hT.b])

        def part2(self, xns, n, Acol, Bcol, ABb, hT, tps):
            for t in range(len(xns)):
                self.part2_tile(xns, t, n, Acol, Bcol, ABb, hT, tps)

        def run_block(self, tiles, n, Acol, Bcol, ABb, hT, tps):
            xns = self.part1(tiles, n)
            self.part2(xns, n, Acol, Bcol, ABb, hT, tps)

    def rms_sq(pss, nchunk, sq):
        for c in range(nchunk):
            act(sq.ap[:, c, :], pss[c].ap, AF.Square, [pss[c].b], [sq.b])

    def rms_fin(nchunk, inv_n, sq, ssb, rstdb):
        for c in range(nchunk):
            mm(ssb.ap, ones_b.ap, sq.ap[:, c, :], c == 0, c == nchunk - 1, [ones_b.b, sq.b], [ssb.b])
        act(rstdb.ap, ssb.ap, AF.Ln, [ssb.b, epsc.b], [rstdb.b], scale=inv_n, bias=epsc.ap)
        act(rstdb.ap, rstdb.ap, AF.Exp, [rstdb.b], [rstdb.b], scale=-0.5)

    def rms_bcast(pss, nchunk, inv_n, sq, ssb, rstdb):
        rms_sq(pss, nchunk, sq)
        rms_fin(nchunk, inv_n, sq, ssb, rstdb)

    def phase_a(x_rows, S):
        m = A.mark()
        win = A.alloc("a_win", (8, 384), BF16)
        dma("sp", win.ap, s_win.ap[:, :, 640:1024], [s_win.b], [win.b], "a_win")
        fr = Front("a")
        hT = [A.alloc("a_hT%d" % i, (8, 512), BF16) for i in range(2)]
        sq = A.alloc("a_sq", (2, 512), BF16)
        rstdb = A.alloc("a_rstdb", (512,), F32)
        ckvn = [A.alloc("a_ckvn%d" % i, (2, 512), BF16) for i in range(2)]
        krot = [A.alloc("a_krot%d" % i, (512,), BF16) for i in range(2)]
        cst = [A.alloc("a_cos%d" % i, (512,), F32) for i in range(2)]
        snt = [A.alloc("a_sin%d" % i, (512,), F32) for i in range(2)]
        t1 = A.alloc("a_t1", (512,), F32)
        t2 = A.alloc("a_t2", (512,), F32)
        tps = [banks[0], banks[7]]
        b1, b2, b3, b4, b5 = banks[1], banks[2], banks[5], banks[6], banks[3]
        nblk = S // 512
        AB = [A1c.b, modc.b]

        def tiles_of(blk):
            return [[(x_rows(blk * 512 + t4 * 128, 128), 0, 128)] for t4 in range(4)]

        xq = {0: fr.part1(tiles_of(0), 128)}
        fr.part2(xq[0], 128, A1c.ap, B1c, AB, hT[0], tps)
        if nblk > 1:
            xq[1] = fr.part1(tiles_of(1), 128)
        for blk in range(nblk):
            h = hT[blk % 2]
            r = blk % 2
            dma("sp", cst[r].ap[0:64, :], cosk_d[:, blk * 512:(blk + 1) * 512], [], [cst[r].b], "a_cos%d" % r)
            dma("sp", snt[r].ap[0:64, :], sink_d[:, blk * 512:(blk + 1) * 512], [], [snt[r].b], "a_sin%d" % r)
            if blk + 2 < nblk:
                xq[blk + 2] = fr.part1(tiles_of(blk + 2), 128)
            for g, (c0, M, bk) in enumerate(((0, 128, b1), (128, 128, b2), (256, 64, b3), (320, 64, b4))):
                for c in range(8):
                    mm(bk.ap[0:M, :], win.ap[:, c, c0:c0 + M], h.ap[:, c, :], c == 0, c == 7, [win.b, h.b], [bk.b])
                if blk + 1 < nblk:
                    fr.part2_tile(xq[blk + 1], g, 128, A1c.ap, B1c, AB, hT[(blk + 1) % 2], tps)
            xq.pop(blk, None)
            rms_sq([b1, b2], 2, sq)
            rms_fin(2, 1.0 / 256, sq, b5, rstdb)
            for c, bk in enumerate((b1, b2)):
                stt(ckvn[r].ap[:, c, :], bk.ap, gcols.ap[:, G_KV + c:G_KV + c + 1], rstdb.ap, ALU.mult, ALU.mult,
                    [bk.b, gcols.b, rstdb.b], [ckvn[r].b])
            tt(t1.ap[0:64, :], b3.ap[0:64, :], cst[r].ap[0:64, :], ALU.mult, [b3.b, cst[r].b], [t1.b])
            tt(t2.ap[0:64, :], b4.ap[0:64, :], snt[r].ap[0:64, :], ALU.mult, [b4.b, snt[r].b], [t2.b])
            tt(krot[r].ap[0:64, :], t1.ap[0:64, :], t2.ap[0:64, :], ALU.add, [t1.b, t2.b], [krot[r].b])
            dma("pool", lat_ckv.ap[:, :, blk * 512:(blk + 1) * 512], ckvn[r].ap, [ckvn[r].b], [lat_ckv.b], "a_lo%d" % r)
            dma("pool", lat_kr.ap[:, blk * 512:(blk + 1) * 512], krot[r].ap[0:64, :], [krot[r].b], [lat_kr.b], "a_ko%d" % r)
        P.barrier()
        A.release(m)

    def phase_b(x_rows, halo_rows, jk):
        m = A.mark()
        win = A.alloc("b_win", (8, 640), BF16)
        dma("sp", win.ap, s_win.ap[:, :, 0:640], [s_win.b], [win.b], "b_win")
        fr = Front("b", nx=4, nxn=4)
        hT = [A.alloc("b_hT%d" % i, (8, 512), BF16) for i in range(2)]
        hTh = A.alloc("b_hTh", (8, 16), BF16)
        sq = A.alloc("b_sq", (3, 512), BF16)
        rstdb = A.alloc("b_rstdb", (512,), F32)
        N = QJ + 16
        uext = A.alloc("b_uext", (2, N), F32)
        sA = A.alloc("b_sA", (2, N), F32)
        sB = A.alloc("b_sB", (2, N), F32)
        invc = A.alloc("b_invc", (2, QJ), F32)
        hmask = A.alloc("b_hmask", (16,), F32)
        tmpw = A.alloc("b_tmpw", (QJ,), F32)
        pooledT = A.alloc("b_pooled", (2, QJ), BF16)
        dma("sp", invc.ap, invc_d[jk], [], [invc.b], "b_invc")
        dma("sp", hmask.ap, hmask_d[jk], [], [hmask.b], "b_hmask")
        tp = banks[0]
        ub = (banks[1], banks[2])
        qb_ = (banks[3], banks[4], banks[5])
        ssb = banks[6]
        b7 = banks[7]
        hl, hr = halo_rows
        fr.run_block([[(hl, 0, 8), (hr, 8, 8)]], 16, A1c.ap, B1c, [A1c.b, modc.b], hTh, [tp])

        AB = [A1c.b, modc.b]

        def tiles_of(blk):
            return [[(x_rows(blk * 512 + t4 * 128, 128), 0, 128)] for t4 in range(4)]

        xq = {0: fr.part1(tiles_of(0), 128)}
        fr.part2(xq[0], 128, A1c.ap, B1c, AB, hT[0], [tp])
        for cu in range(2):
            for c in range(8):
                mm(b7.ap[:, cu * 16:(cu + 1) * 16], win.ap[:, c, cu * 128:(cu + 1) * 128], hTh.ap[:, c, :],
                   c == 0, c == 7, [win.b, hTh.b], [b7.b])
        for cu in range(2):
            tt(uext.ap[:, cu, 0:8], b7.ap[:, cu * 16:cu * 16 + 8], hmask.ap[:, 0:8], ALU.mult,
               [b7.b, hmask.b], [uext.b])
            tt(uext.ap[:, cu, 8 + QJ:16 + QJ], b7.ap[:, cu * 16 + 8:cu * 16 + 16], hmask.ap[:, 8:16], ALU.mult,
               [b7.b, hmask.b], [uext.b])
        for blk in range(NQB):
            h = hT[blk % 2]
            if blk + 1 < NQB:
                xq[blk + 1] = fr.part1(tiles_of(blk + 1), 128)
            groups = [(ub[0], 0), (ub[1], 128), (qb_[0], 256), (qb_[1], 384), (qb_[2], 512)]
            for g, (bk, c0) in enumerate(groups):
                for c in range(8):
                    mm(bk.ap, win.ap[:, c, c0:c0 + 128], h.ap[:, c, :], c == 0, c == 7, [win.b, h.b], [bk.b])
                if g < 2:
                    act(uext.ap[:, g, 8 + blk * 512:8 + (blk + 1) * 512], bk.ap, AF.Copy, [bk.b], [uext.b])
                if blk + 1 < NQB and g < 4:
                    fr.part2_tile(xq[blk + 1], g, 128, A1c.ap, B1c, AB, hT[(blk + 1) % 2], [tp])
            xq.pop(blk, None)
            rms_bcast(list(qb_), 3, 1.0 / 384, sq, ssb, rstdb)
            for cq in range(3):
                stt(cqT.ap[:, cq, blk * 512:(blk + 1) * 512], qb_[cq].ap, gcols.ap[:, G_Q + cq:G_Q + cq + 1],
                    rstdb.ap, ALU.mult, ALU.mult, [qb_[cq].b, gcols.b, rstdb.b], [cqT.b])
        u = uext
        tt(sA.ap[:, :, 1:N], u.ap[:, :, 1:N], u.ap[:, :, 0:N - 1], ALU.add, [u.b], [sA.b])
        wins = []

        def pooled(src, rows, c, sh):
            tt(tmpw.ap[rows, :], src.ap[rows, c, sh:sh + QJ], invc.ap[rows, c, :], ALU.mult, [src.b, invc.b], [tmpw.b])
            tt(pooledT.ap[rows, c, :], tmpw.ap[rows, :], u.ap[rows, c, 8:8 + QJ], ALU.subtract,
               [tmpw.b, u.b], [pooledT.b])

        lo, hi = slice(0, 64), slice(64, 128)
        pooled(sA, lo, 0, 8)
        tt(sB.ap[:, :, 3:N], sA.ap[:, :, 3:N], sA.ap[:, :, 1:N - 2], ALU.add, [sA.b], [sB.b])
        pooled(sB, hi, 0, 9)
        tt(sA.ap[:, :, 7:N], sB.ap[:, :, 7:N], sB.ap[:, :, 3:N - 4], ALU.add, [sB.b], [sA.b])
        pooled(sA, lo, 1, 11)
        tt(sB.ap[:, :, 15:N], sA.ap[:, :, 15:N], sA.ap[:, :, 7:N - 8], ALU.add, [sA.b], [sB.b])
        pooled(sB, hi, 1, 15)
        for blk in range(NQB):
            for c in range(2):
                mm(b7.ap, pwbd.ap[:, c, :], pooledT.ap[:, c, blk * 512:(blk + 1) * 512], True, True,
                   [pwbd.b, pooledT.b], [b7.b])
                ts(poolT.ap[:, c, blk * 512:(blk + 1) * 512], b7.ap, gcols.ap[:, G_PS + c:G_PS + c + 1], None,
                   ALU.mult, None, [b7.b, gcols.b], [poolT.b])
        P.barrier()
        A.release(m)

    def phase_c(S, cos_src, sin_src):
        m = A.mark()
        wuq = A.alloc("c_wuq", (3, 1536), BF16)
        wukv = A.alloc("c_wukv", (2, 1536), BF16)
        dma("sp", wuq.ap, s_wuq.ap, [s_wuq.b], [wuq.b], "c_wuq")
        dma("sp", wukv.ap, s_wukv.ap, [s_wukv.b], [wukv.b], "c_wukv")
        cosq = A.alloc("c_cosq", (QJ,), F32)
        sinq = A.alloc("c_sinq", (QJ,), F32)
        dma("sp", cosq.ap[0:64, :], cos_src, [], [cosq.b], "c_cosq")
        dma("sp", sinq.ap[0:64, :], sin_src, [], [sinq.b], "c_sinq")
        ts(cosq.ap[0:64, :], cosq.ap[0:64, :], QSCALE, None, ALU.mult, None, [cosq.b], [cosq.b])
        ts(sinq.ap[0:64, :], sinq.ap[0:64, :], QSCALE, None, ALU.mult, None, [sinq.b], [sinq.b])
        Qn = [A.alloc("c_Qn%d" % i, (QJ,), BF16) for i in range(2)]
        Qr = [A.alloc("c_Qr%d" % i, (QJ,), BF16) for i in range(2)]
        for i in range(2):
            memset(Qr[i].ap[64:128, :], 0.0, [Qr[i].b])
        NR = 3
        ckb = [A.alloc("c_ckb%d" % i, (2, 512), BF16) for i in range(NR)]
        krb = [A.alloc("c_krb%d" % i, (512,), BF16) for i in range(NR)]
        for i in range(NR):
            memset(krb[i].ap[64:128, :], 0.0, [krb[i].b])
        knb = [A.alloc("c_knb%d" % i, (512,), BF16) for i in range(2)]
        vb = [A.alloc("c_vb%d" % i, (4, 128), BF16) for i in range(2)]
        NP = 2 * NQB + 4
        Pt = [A.alloc("c_P%d" % i, (512,), BF16) for i in range(NP)]
        tpair = [A.alloc("c_tp%d" % i, (512,), BF16) for i in range(2)]
        acc = [A.alloc("c_acc%d" % i, (512,), F32) for i in range(NQB)]
        rinv = A.alloc("c_rinv", (512,), F32)
        t1 = A.alloc("c_t1", (512,), F32)
        t2 = A.alloc("c_t2", (512,), F32)
        Ob = banks[0:NQB]
        Sb = banks[4:7]
        b7 = banks[7]
        NKB = S // 512
        L = 2

        def load_lat(kb, r):
            dma("sp", ckb[r].ap, lat_ckv.ap[:, :, kb * 512:(kb + 1) * 512], [lat_ckv.b], [ckb[r].b], "c_ckb%d" % r)
            dma("sp", krb[r].ap[0:64, :], lat_kr.ap[:, kb * 512:(kb + 1) * 512], [lat_kr.b], [krb[r].b], "c_krb%d" % r)

        def gen_k(h, kb):
            r, k2 = kb % NR, kb % 2
            for c in range(2):
                mm(b7.ap, wukv.ap[:, c, h * 256:h * 256 + 128], ckb[r].ap[:, c, :], c == 0, c == 1,
                   [wukv.b, ckb[r].b], [b7.b])
            act(knb[k2].ap, b7.ap, AF.Copy, [b7.b], [knb[k2].b])

        def gen_v(h, kb):
            r, k2 = kb % NR, kb % 2
            b7v = b7.ap.rearrange("p (t d) -> p t d", t=4)
            for t in range(4):
                for c in range(2):
                    mm(b7v[:, t, :], ckb[r].ap[:, c, t * 128:(t + 1) * 128], wukv.ap[:, c, h * 256 + 128:h * 256 + 256],
                       c == 0, c == 1, [wukv.b, ckb[r].b], [b7.b])
            cp(vb[k2].ap, b7v, [b7.b], [vb[k2].b])

        def gen_q(h):
            qi = h % 2
            bl = [Sb[0], Sb[1], Sb[2], b7]
            k = 0
            for qb in range(NQB):
                cs = slice(qb * 512, (qb + 1) * 512)
                bk = bl[k % 4]; k += 1
                for c in range(3):
                    mm(bk.ap, wuq.ap[:, c, h * 192:h * 192 + 128], cqT.ap[:, c, cs], c == 0, c == 2,
                       [wuq.b, cqT.b], [bk.b])
                act(Qn[qi].ap[:, cs], bk.ap, AF.Copy, [bk.b], [Qn[qi].b], scale=QSCALE)
                bk1 = bl[k % 4]; k += 1
                for c in range(3):
                    mm(bk1.ap[0:64, :], wuq.ap[:, c, h * 192 + 128:h * 192 + 192], cqT.ap[:, c, cs], c == 0, c == 2,
                       [wuq.b, cqT.b], [bk1.b])
                bk2 = bl[k % 4]; k += 1
                for c in range(3):
                    mm(bk2.ap[0:64, :], wuq.ap[:, c, 1152 + h * 64:1152 + (h + 1) * 64], cqT.ap[:, c, cs], c == 0, c == 2,
                       [wuq.b, cqT.b], [bk2.b])
                tt(t1.ap[0:64, :], bk1.ap[0:64, :], cosq.ap[0:64, cs], ALU.mult, [bk1.b, cosq.b], [t1.b])
                tt(t2.ap[0:64, :], bk2.ap[0:64, :], sinq.ap[0:64, cs], ALU.mult, [bk2.b, sinq.b], [t2.b])
                tt(Qr[qi].ap[0:64, cs], t1.ap[0:64, :], t2.ap[0:64, :], ALU.add, [t1.b, t2.b], [Qr[qi].b])

        for h in range(NH):
            qi = h % 2
            load_lat(0, 0)
            if NKB > 1:
                load_lat(1, 1)
            gen_q(h)
            gen_k(h, 0)
            gen_v(h, 0)
            items = [(kb, t, qb) for kb in range(NKB) for t in range(4) for qb in range(NQB)]
            n = len(items)
            per_kb = 4 * NQB
            for i in range(n + L):
                if i < n:
                    kb, t, qb = items[i]
                    if i % per_kb == 0:
                        if kb + 2 < NKB:
                            load_lat(kb + 2, (kb + 2) % NR)
                        if kb + 1 < NKB:
                            gen_k(h, kb + 1)
                    if i % per_kb == per_kb // 2 and kb + 1 < NKB:
                        gen_v(h, kb + 1)
                    r, k2 = kb % NR, kb % 2
                    cs = slice(qb * 512, (qb + 1) * 512)
                    sb_ = Sb[i % 3]
                    mm(sb_.ap, knb[k2].ap[:, t * 128:(t + 1) * 128], Qn[qi].ap[:, cs], True, False,
                       [knb[k2].b, Qn[qi].b], [sb_.b])
                    mm(sb_.ap, krb[r].ap[:, t * 128:(t + 1) * 128], Qr[qi].ap[:, cs], False, True,
                       [krb[r].b, Qr[qi].b], [sb_.b])
                    pt = Pt[i % NP]
                    act(pt.ap, sb_.ap, AF.Exp, [sb_.b], [pt.b])
                if i >= L:
                    i2 = i - L
                    kb, t, qb = items[i2]
                    k2 = kb % 2
                    kt = kb * 4 + t
                    pt = Pt[i2 % NP]
                    mm(Ob[qb].ap, vb[k2].ap[:, t, :], pt.ap, kt == 0, kt == NKB * 4 - 1, [vb[k2].b, pt.b], [Ob[qb].b])
                    if kt % 2 == 1:
                        pprev = Pt[(i2 - NQB) % NP]
                        if kt == 1:
                            tt(acc[qb].ap, pprev.ap, pt.ap, ALU.add, [pprev.b, pt.b], [acc[qb].b])
                        else:
                            tpb = tpair[(kt // 2) % 2]
                            tt(tpb.ap, pprev.ap, pt.ap, ALU.add, [pprev.b, pt.b], [tpb.b])
                            tt(acc[qb].ap, acc[qb].ap, tpb.ap, ALU.add, [acc[qb].b, tpb.b], [acc[qb].b])
            for qb in range(NQB):
                cs = slice(qb * 512, (qb + 1) * 512)
                mm(b7.ap, ones_f.ap, acc[qb].ap, True, True, [ones_f.b, acc[qb].b], [b7.b])
                act(rinv.ap, b7.ap, AF.Ln, [b7.b], [rinv.b])
                act(rinv.ap, rinv.ap, AF.Exp, [rinv.b], [rinv.b], scale=-1.0)
                tt(attnT.ap[:, h, cs], Ob[qb].ap, rinv.ap, ALU.mult, [Ob[qb].b, rinv.b], [attnT.b])
        P.barrier()
        A.release(m)

    def phase_d(x_rows, y_rows):
        m = A.mark()
        A.release(CQ_MARK)
        TB = 512
        NB = QJ // TB
        wout = A.alloc("d_wout", (8, D), BF16)
        wd = A.alloc("d_wd", (NFF, D), BF16)
        xt = [A.alloc("d_xt%d" % t, (D,), F32) for t in range(4)]
        mg = [A.alloc("d_mg%d" % i, (D,), F32) for i in range(2)]
        junks = [A.alloc("d_junk%d" % i, (D,), BF16) for i in range(2)]
        junk = junks[0]
        xn = [A.alloc("d_xn%d" % i, (D,), BF16) for i in range(2)]
        sqa = [A.alloc("d_sqa%d" % i, (8, 128), BF16) for i in range(2)]
        scs = [A.alloc("d_sc%d" % i, (4, 4), F32) for i in range(2)]
        sf = A.alloc("d_sf", (4,), F32)
        h2T = [A.alloc("d_h2T%d" % i, (8, TB), BF16) for i in range(2)]
        actT = A.alloc("d_actT", (NFF, TB), BF16)
        sg = [A.alloc("d_sg%d" % i, (TB,), F32) for i in range(2)]
        fst = [A.alloc("d_fst%d" % i, (D,), F32) for i in range(2)]
        x1b = mg
        NRW = 3
        wgu = [A.alloc("d_wgu%d" % i, (2, 8, 128), BF16) for i in range(NRW)]
        dma("sp", wout.ap, s_wout.ap, [s_wout.b], [wout.b], "d_wout")
        tps = [banks[0], banks[7]]
        b7 = banks[7]
        MPb, MAb = banks[1], banks[2]
        b_spill = [Buf("x1spill0"), Buf("x1spill1")]
        state = {"yi": 0, "wd_loaded": False}

        def d1_stages(b):
            st_ = b % 2
            sc = scs[st_]
            h2 = h2T[st_]
            stages = []

            def s1():
                for t in range(4):
                    q0 = b * TB + t * 128
                    qs = slice(q0, q0 + 128)
                    dma("sp", xt[t].ap, x_rows(q0, 128), [], [xt[t].b], "d_xt%d" % t)
                    sq_ = sqa[t % 2]
                    tt(sq_.ap[:, 0:2, :], poolT.ap[:, :, qs], poolT.ap[:, :, qs], ALU.mult, [poolT.b], [sq_.b])
                    tt(sq_.ap[:, 2:8, :], attnT.ap[:, :, qs], attnT.ap[:, :, qs], ALU.mult, [attnT.b], [sq_.b])
                    for c in range(2):
                        mm(b7.ap[:, t:t + 1], sq_.ap[:, c, :], ones_b.ap[:, 0:1], c == 0, c == 1, [sq_.b, ones_b.b], [b7.b])
                    for c in range(6):
                        mm(b7.ap[:, 4 + t:5 + t], sq_.ap[:, 2 + c, :], ones_b.ap[:, 0:1], c == 0, c == 5,
                           [sq_.b, ones_b.b], [b7.b])
                if not state["wd_loaded"]:
                    dma("sp", wd.ap, s_wd.ap, [s_wd.b], [wd.b], "d_wd")
                    state["wd_loaded"] = True
                act(sc.ap[:, 0, :], b7.ap[:, 0:4], AF.Ln, [b7.b, epsc.b], [sc.b], scale=1.0 / 256, bias=epsc.ap)
                act(sc.ap[:, 1, :], b7.ap[:, 4:8], AF.Ln, [b7.b, epsc.b], [sc.b], scale=1.0 / 768, bias=epsc.ap)
                act(sc.ap[:, 0:2, :], sc.ap[:, 0:2, :], AF.Exp, [sc.b], [sc.b], scale=-0.5)
            stages.append(s1)

            def mk_combo(t, half):
                def f():
                    q0 = b * TB + t * 128
                    qs = slice(q0, q0 + 128)
                    hs = slice(half * 512, (half + 1) * 512)
                    g_ = mg[t % 2]
                    for c in range(2):
                        mm(MPb.ap, poolT.ap[:, c, qs], wout.ap[:, c, hs], c == 0, c == 1, [poolT.b, wout.b], [MPb.b])
                    for c in range(6):
                        mm(MAb.ap, attnT.ap[:, c, qs], wout.ap[:, 2 + c, hs], c == 0, c == 5, [attnT.b, wout.b], [MAb.b])
                    ts(g_.ap[:, hs], MPb.ap, sc.ap[:, 0, t:t + 1], None, ALU.mult, None, [MPb.b, sc.b], [g_.b])
                    stt(g_.ap[:, hs], MAb.ap, sc.ap[:, 1, t:t + 1], g_.ap[:, hs], ALU.mult, ALU.add,
                        [MAb.b, sc.b, g_.b], [g_.b])
                return f

            def mk_s3a(t0):
                def f():
                    for t in (t0, t0 + 1):
                        act(junks[t % 2].ap, mg[t % 2].ap, AF.Square, [mg[t % 2].b], [junks[t % 2].b, sc.b],
                            scale=float(D) ** -0.5, accum_out=sc.ap[:, 2, t:t + 1])
                return f

            def mk_s4(t0):
                def f():
                    act(sc.ap[:, 2, t0:t0 + 2], sc.ap[:, 2, t0:t0 + 2], AF.Ln, [sc.b, epsc.b], [sc.b], bias=epsc.ap)
                    act(sc.ap[:, 2, t0:t0 + 2], sc.ap[:, 2, t0:t0 + 2], AF.Exp, [sc.b], [sc.b], scale=-0.5)
                    for t in (t0, t0 + 1):
                        g_ = mg[t % 2]
                        q0 = b * TB + t * 128
                        stt(g_.ap, g_.ap, sc.ap[:, 2, t:t + 1], G1b.ap, ALU.mult, ALU.mult, [g_.b, sc.b, G1b.b], [g_.b])
                        tt(xt[t].ap, g_.ap, xt[t].ap, ALU.add, [g_.b, xt[t].b], [xt[t].b])
                        dma("pool", y_rows(q0, 128), xt[t].ap, [xt[t].b], [b_spill[st_]], "d_sp%d" % t)
                return f

            def mk_s5a(t0):
                def f():
                    for t in (t0, t0 + 1):
                        act(junks[t % 2].ap, xt[t].ap, AF.Square, [xt[t].b], [junks[t % 2].b, sc.b],
                            scale=float(D) ** -0.5, accum_out=sc.ap[:, 3, t:t + 1])
                return f

            def mk_s5b(t0):
                def f():
                    act(sc.ap[:, 3, t0:t0 + 2], sc.ap[:, 3, t0:t0 + 2], AF.Ln, [sc.b, epsc.b], [sc.b], bias=epsc.ap)
                    act(sc.ap[:, 3, t0:t0 + 2], sc.ap[:, 3, t0:t0 + 2], AF.Exp, [sc.b], [sc.b], scale=-0.5)
                return f

            def mk_s5c(t0):
                def f():
                    for t in (t0, t0 + 1):
                        act(xn[t % 2].ap, xt[t].ap, AF.Copy, [xt[t].b, sc.b], [xn[t % 2].b], scale=sc.ap[:, 3, t:t + 1])
                return f

            def mk_tr(t):
                def f():
                    tp = tps[t % 2]
                    tpv = tp.ap.bitcast(BF16).rearrange("p (c n) -> p c n", c=8)
                    for c in range(8):
                        tr(tpv[:, c, :], xn[t % 2].ap[:, c * 128:(c + 1) * 128], idb.ap, [xn[t % 2].b, idb.b], [tp.b])
                    for c in range(8):
                        ts(h2.ap[:, c, t * 128:(t + 1) * 128], tpv[:, c, :], A2c.ap[:, c:c + 1], B2c[:, c:c + 1],
                           ALU.mult, ALU.add, [tp.b, A2c.b, modc.b], [h2.b])
                return f

            for t0 in (0, 2):
                for t in (t0, t0 + 1):
                    for half in range(2):
                        stages.append(mk_combo(t, half))
                stages.append(mk_s3a(t0))
                stages.append(mk_s4(t0))
                stages.append(mk_s5a(t0))
                stages.append(mk_s5b(t0))
                stages.append(mk_s5c(t0))
                stages.append(mk_tr(t0))
                stages.append(mk_tr(t0 + 1))
            return stages

        def d2_unit(b, j, u):
            h2 = h2T[b % 2]
            r = u % NRW
            dma("sp", wgu[r].ap, s_wgu.ap[j], [s_wgu.b], [wgu[r].b], "d_wgu%d" % r)
            gb, ub = banks[3 + 2 * (u % 2)], banks[4 + 2 * (u % 2)]
            for c in range(8):
                mm(gb.ap, wgu[r].ap[:, 0, c, :], h2.ap[:, c, :], c == 0, c == 7, [wgu[r].b, h2.b], [gb.b])
            for c in range(8):
                mm(ub.ap, wgu[r].ap[:, 1, c, :], h2.ap[:, c, :], c == 0, c == 7, [wgu[r].b, h2.b], [ub.b])
            s_ = sg[u % 2]
            act(s_.ap, gb.ap, AF.Silu, [gb.b], [s_.b])
            tt(actT.ap[:, j, :], s_.ap, ub.ap, ALU.mult, [s_.b, ub.b], [actT.b])

        def d3(b):
            for t in range(4):
                q0 = b * TB + t * 128
                yi = state["yi"]
                f_ = fst[yi % 2]
                x1_ = x1b[yi % 2]
                dma("sp", x1_.ap, y_rows(q0, 128), [b_spill[b % 2]], [x1_.b], "d_x1b%d" % (yi % 2))
                for half in range(2):
                    hs = slice(half * 512, (half + 1) * 512)
                    fb = banks[1 + half]
                    for j in range(NFF):
                        mm(fb.ap, actT.ap[:, j, t * 128:(t + 1) * 128], wd.ap[:, j, hs], j == 0, j == NFF - 1,
                           [actT.b, wd.b], [fb.b])
                    act(f_.ap[:, hs], fb.ap, AF.Copy, [fb.b], [f_.b])
                k = yi % 4
                act(junk.ap, f_.ap, AF.Square, [f_.b], [junk.b, sf.b], scale=float(D) ** -0.5, accum_out=sf.ap[:, k:k + 1])
                act(sf.ap[:, k:k + 1], sf.ap[:, k:k + 1], AF.Ln, [sf.b, epsc.b], [sf.b], bias=epsc.ap)
                act(sf.ap[:, k:k + 1], sf.ap[:, k:k + 1], AF.Exp, [sf.b], [sf.b], scale=-0.5)
                stt(f_.ap, f_.ap, sf.ap[:, k:k + 1], G2b.ap, ALU.mult, ALU.mult, [f_.b, sf.b, G2b.b], [f_.b])
                tt(f_.ap, f_.ap, x1_.ap, ALU.add, [f_.b, x1_.b], [f_.b])
                dma("pool", y_rows(q0, 128), f_.ap, [f_.b, x1_.b], [b_yout], "d_y%d" % (yi % 2), is_out=True)
                state["yi"] = yi + 1

        for f in d1_stages(0):
            if f:
                f()
        u = 0
        for b in range(NB):
            stages = d1_stages(b + 1) if b + 1 < NB else []
            for j in range(NFF):
                d2_unit(b, j, u)
                u += 1
                if stages:
                    f = stages.pop(0)
                    if f:
                        f()
            while stages:
                f = stages.pop(0)
                if f:
                    f()
            d3(b)
        P.barrier()
        A.release(m)

    m0 = A.mark()
    dma("sp", gcols.ap, gcols_d, [], [gcols.b], "gcols")
    dma("sp", badac.ap, badac_d, [], [badac.b], "badac")
    mod_both()
    init()
    P.barrier()
    A.release(m0)
    phase_a(lambda r0, n: xs_d[r0:r0 + n, :], SS)
    init_b()
    phase_b(lambda r0, n: xsq_d[8 + r0:8 + r0 + n, :], (xsq_d[0:8, :], xsq_d[8 + QJ:16 + QJ, :]), 0)
    phase_c(SS, cosq_d, sinq_d)
    phase_d(lambda r0, n: xsq_d[8 + r0:8 + r0 + n, :], lambda r0, n: ys_d[r0:r0 + n, :])
    load_mod1()
    phase_a(lambda r0, n: xp_d[8 + r0:8 + r0 + n, :], SP)
    for j in range(NJP):
        j0 = j * QJ
        phase_b(lambda r0, n, j0=j0: xp_d[8 + j0 + r0:8 + j0 + r0 + n, :],
                (xp_d[j0:j0 + 8, :], xp_d[8 + j0 + QJ:16 + j0 + QJ, :]), 1 + j)
        phase_c(SP, cosk_d[:, j0:j0 + QJ], sink_d[:, j0:j0 + QJ])
        phase_d(lambda r0, n, j0=j0: xp_d[8 + j0 + r0:8 + j0 + r0 + n, :],
                lambda r0, n, j0=j0: yp_d[j0 + r0:j0 + r0 + n, :])

    P.finalize(nc, st)
    with nc.Block() as block:
        @block.tensor
        def _(e):
            P.emit("pe", e)

        @block.scalar
        def _(e):
            P.emit("act", e)

        @block.vector
        def _(e):
            P.emit("dve", e)

        @block.gpsimd
        def _(e):
            P.emit("pool", e)

        @block.sync
        def _(e):
            P.emit("sp", e)
            P.final_waits(e)
    st.close()
    build.stats = {k: len(v) for k, v in P.q.items()}
    build.peak = A.peak
    return nc


def rope_tables(S):
    inv_freq = (np.float32(10000.0) ** (-(np.arange(0, 64, 2, dtype=np.float32)) / np.float32(64))).astype(np.float32)
    ang = (np.arange(S, dtype=np.float32)[:, None] * inv_freq[None, :]).astype(np.float32)
    cos = np.cos(ang).astype(np.float32).T
    sin = np.sin(ang).astype(np.float32).T
    cos2 = np.concatenate([cos, cos], 0)
    sins = np.concatenate([-sin, sin], 0)
    return np.ascontiguousarray(cos2), np.ascontiguousarray(sins)


def pool_tables(S, start, n):
    t = np.arange(start, start + n)
    invc = np.zeros((128, 2, n), np.float32)
    for g, w in enumerate((2, 4, 8, 16)):
        lo = np.clip(t - w // 2, 0, S)
        hi = np.clip(t + w - w // 2, 0, S)
        v = (1.0 / (hi - lo)).astype(np.float32)
        c, half = g // 2, g % 2
        invc[half * 64:(half + 1) * 64, c, :] = v[None, :]
    hp = np.concatenate([np.arange(start - 8, start), np.arange(start + n, start + n + 8)])
    hm = ((hp >= 0) & (hp < S)).astype(np.float32)
    return invc, np.ascontiguousarray(np.broadcast_to(hm[None, :], (128, 16)))


def col_layout(v):
    return np.ascontiguousarray(v.reshape(-1, 128).T)


_NC_CACHE = {}


def make_in_maps(cfg, inp):
    SP, SS, QJ, NJP, SMAX = cfg.SP, cfg.SS, cfg.QJ, cfg.NJP, cfg.SMAX
    f = lambda a: np.ascontiguousarray(np.asarray(a, dtype=np.float32))
    w_in = f(inp["w_in"][0])
    w_in_x = np.concatenate([w_in, w_in[:, 928:960], w_in[:, 896:928]], axis=1)
    w_uq = f(inp["w_uq"][0])
    sw = []
    for h in range(NH):
        b = h * 192 + 128
        sw += [w_uq[:, b + 32:b + 64], w_uq[:, b:b + 32]]
    w_uq_x = np.concatenate([w_uq] + sw, axis=1)
    pool_w = f(inp["pool_w"][0])
    pw_bd = np.zeros((2, 128, 128), np.float32)
    for g in range(4):
        c, hf = g // 2, g % 2
        pw_bd[c, hf * 64:(hf + 1) * 64, hf * 64:(hf + 1) * 64] = pool_w[g]
    gcols = np.concatenate([
        col_layout(f(inp["g_mix_pre"][0])), col_layout(f(inp["g_ffn_pre"][0])), col_layout(f(inp["g_q_a"][0])),
        col_layout(f(inp["g_kv_a"][0])), col_layout(f(inp["pool_scale"][0])),
        col_layout(np.concatenate([f(inp["g_pool_out"][0]), f(inp["g_attn_out"][0])]))], axis=1)
    gpost = np.stack([f(inp["g_mix_post"][0]), f(inp["g_ffn_post"][0])])
    b_ada = f(inp["b_ada"][0])
    cosk, sink = rope_tables(SMAX)
    xs = f(inp["x_sample"][0])
    xs_pad = np.concatenate([np.zeros((8, D), np.float32), xs, np.zeros((8, D), np.float32)], 0)
    shared = {
        "xs": xs, "w_ada": f(inp["w_ada"][0]), "b_ada_col": col_layout(b_ada), "b_ada_row": b_ada,
        "gcols": np.ascontiguousarray(gcols), "gpost": np.ascontiguousarray(gpost), "w_in_x": np.ascontiguousarray(w_in_x),
        "w_uq_x": np.ascontiguousarray(w_uq_x), "w_ukv": f(inp["w_ukv"][0]), "w_out": f(inp["w_out"][0]),
        "w_gate": f(inp["w_gate"][0]), "w_up": f(inp["w_up"][0]), "w_down": f(inp["w_down"][0]), "pw_bd": pw_bd,
        "cosk": cosk, "sink": sink, "ident": np.eye(128, dtype=np.float32),
    }
    ptabs = [pool_tables(SP, j * QJ, QJ) for j in range(NJP)]
    maps = []
    for c in range(NCORES):
        xp = f(inp["x_prompt"][c])
        st_ = pool_tables(SS, c * QJ, QJ)
        m = dict(shared)
        m["xp"] = np.concatenate([np.zeros((8, D), np.float32), xp, np.zeros((8, D), np.float32)], 0)
        m["xsq"] = np.ascontiguousarray(xs_pad[c * QJ:c * QJ + QJ + 16])
        m["cvec"] = np.stack([col_layout(f(inp["c_sample"][0])), col_layout(f(inp["c_prompt"][c]))])
        m["cosq"] = np.ascontiguousarray(cosk[:, c * QJ:(c + 1) * QJ])
        m["sinq"] = np.ascontiguousarray(sink[:, c * QJ:(c + 1) * QJ])
        m["invc"] = np.stack([st_[0]] + [p[0] for p in ptabs])
        m["hmask"] = np.stack([st_[1]] + [p[1] for p in ptabs])
        maps.append(m)
    return maps


def run(cfg, inp, trace=False):
    key = (cfg.SP, cfg.SS, cfg.NQB)
    if key not in _NC_CACHE:
        _NC_CACHE[key] = build(cfg)
    nc = _NC_CACHE[key]
    maps = make_in_maps(cfg, inp)
    res = run_bass_kernel_spmd(nc, maps, core_ids=list(range(NCORES)), trace=trace)
    yp = np.stack([res.results[c]["yp"] for c in range(NCORES)], 0)
    ys = np.concatenate([res.results[c]["ys"] for c in range(NCORES)], 0)[None]
    return (yp.astype(np.float32), ys.astype(np.float32)), res


def kernel(**inputs):
    cfg = Cfg()
    out, _ = run(cfg, inputs)
    return out
```
